# Optimizing a Trainium2 kernel written in Bass

```python
import jax
import jax.numpy as jnp
from jax import lax
import numpy as np

D_MODEL = 2048
BATCH = 1
SEQ = 8192
DEPTH = 2

PLE_DIM = 256
RMS_EPS = 1e-6

MLSTM_HEADS = 4
MLSTM_HEAD_DIM = 256
MLSTM_W = MLSTM_HEADS * MLSTM_HEAD_DIM
MLSTM_CONV = 4
MLSTM_CHUNK = 64
GATE_SOFTCAP = 15.0

RWKV_HEADS = 8
RWKV_HEAD_DIM = 64
RWKV_W = RWKV_HEADS * RWKV_HEAD_DIM
DECAY_LORA = 96
AAA_LORA = 96
GATE_LORA = 256
RWKV_GN_EPS = 64e-5

S5_GROUP = 16
S5_GROUPS = 32
S5_W = S5_GROUP * S5_GROUPS
S5_STATE = 64

FFN_HIDDEN = ((8 * D_MODEL + 3 * 256 - 1) // (3 * 256)) * 256

M_IN = 4 * MLSTM_W + 2 * MLSTM_HEADS
RWKV_IN = 3 * RWKV_W + DECAY_LORA + AAA_LORA + GATE_LORA
N_BRANCH = 3
N_IN = M_IN + RWKV_IN + S5_W + N_BRANCH * D_MODEL
IN_SPLIT = (M_IN, M_IN + RWKV_IN, M_IN + RWKV_IN + S5_W)
RWKV_SPLIT = (RWKV_W, 2 * RWKV_W, 3 * RWKV_W, 3 * RWKV_W + DECAY_LORA, 3 * RWKV_W + DECAY_LORA + AAA_LORA)

kernel_name = 'hybrid_mlstm_rwkv7_s5_gated'


def _rmsnorm(x, g):
    xf = x.astype(jnp.float32)
    y = xf * lax.rsqrt(jnp.mean(xf * xf, axis=-1, keepdims=True) + RMS_EPS)
    return (y * g.astype(jnp.float32)).astype(x.dtype)


def _shift(x, n):
    return jnp.pad(x, ((0, 0), (n, 0), (0, 0)))[:, :x.shape[1]]


def _causal_dwconv(x, w):
    out = x * w[0]
    for j in range(1, w.shape[0]):
        out = out + w[j] * _shift(x, j)
    return out


def _token_shift(x, mu):
    return x + (_shift(x, 1) - x) * mu


def _softcap(z):
    return GATE_SOFTCAP * jnp.tanh(z / GATE_SOFTCAP)


def _mlstm(q, k, v, o, ig, fg, norm_g):
    B, T, _ = q.shape
    H, dh, L = MLSTM_HEADS, MLSTM_HEAD_DIM, MLSTM_CHUNK
    nc = T // L
    f32 = jnp.float32

    def to_chunks(z):
        return z.astype(f32).reshape(B, nc, L, H, dh).transpose(1, 0, 3, 2, 4)

    def gate_chunks(z):
        return z.reshape(B, nc, L, H).transpose(1, 0, 3, 2)

    qc = to_chunks(q) * (dh ** -0.5)
    kc = to_chunks(k)
    vc = to_chunks(v)
    ic = gate_chunks(_softcap(ig.astype(f32)))
    logf = jax.nn.log_sigmoid(_softcap(fg.astype(f32)))
    bc = jnp.cumsum(gate_chunks(logf), axis=-1)
    causal = jnp.tril(jnp.ones((L, L), dtype=bool))

    def step(carry, inp):
        C, n, m = carry
        qb, kb, vb, ib, bb = inp
        dmat = bb[..., :, None] - bb[..., None, :] + ib[..., None, :]
        dmat = jnp.where(causal, dmat, -jnp.inf)
        inter = bb + m[..., None]
        m_t = jnp.maximum(inter, jnp.max(dmat, axis=-1))
        s = jnp.einsum('bhtd,bhsd->bhts', qb, kb) * jnp.exp(dmat - m_t[..., None])
        inter_w = jnp.exp(inter - m_t)
        num = jnp.einsum('bhts,bhsd->bhtd', s, vb) + inter_w[..., None] * jnp.einsum('bhtk,bhvk->bhtv', qb, C)
        den = jnp.sum(s, axis=-1) + inter_w * jnp.einsum('bhtk,bhk->bht', qb, n)
        h = num / jnp.maximum(jnp.abs(den), jnp.exp(-m_t))[..., None]
        b_end = bb[..., -1]
        wlog = b_end[..., None] - bb + ib
        m_new = jnp.maximum(b_end + m, jnp.max(wlog, axis=-1))
        decay = jnp.exp(b_end + m - m_new)
        ws = jnp.exp(wlog - m_new[..., None])
        C_new = decay[..., None, None] * C + jnp.einsum('bhs,bhsv,bhsk->bhvk', ws, vb, kb)
        n_new = decay[..., None] * n + jnp.einsum('bhs,bhsk->bhk', ws, kb)
        return (C_new, n_new, m_new), h

    carry0 = (jnp.zeros((B, H, dh, dh), f32), jnp.zeros((B, H, dh), f32), jnp.zeros((B, H), f32))
    _, h = lax.scan(step, carry0, (qc, kc, vc, ic, bc))
    h = h.transpose(1, 0, 3, 2, 4).reshape(B, T, H, dh)
    h = h * lax.rsqrt(jnp.mean(h * h, axis=-1, keepdims=True) + RMS_EPS)
    h = h.reshape(B, T, MLSTM_W) * norm_g.astype(f32)
    return jax.nn.sigmoid(o.astype(f32)) * h


def _rwkv7(r, k, v, wl, al, gl, w0, w2, a0, a2, g2, k_k, k_a, r_k, ln_g, ln_b):
    B, T, _ = r.shape
    H, dh = RWKV_HEADS, RWKV_HEAD_DIM
    f32 = jnp.float32
    r, k, v = r.astype(f32), k.astype(f32), v.astype(f32)
    w = -jax.nn.softplus(-(w0 + jnp.tanh(wl.astype(f32)) @ w2)) - 0.5
    decay = jnp.exp(-jnp.exp(w))
    a = jax.nn.sigmoid(a0 + al.astype(f32) @ a2)
    g = jax.nn.sigmoid(gl.astype(f32)) @ g2
    kk = (k * k_k).reshape(B, T, H, dh)
    kk = kk / jnp.maximum(jnp.sqrt(jnp.sum(kk * kk, axis=-1, keepdims=True)), 1e-12)
    k = k * (1.0 + (a - 1.0) * k_a)

    def heads(z):
        return z.reshape(B, T, H, dh).transpose(1, 0, 2, 3)

    kk_t = kk.transpose(1, 0, 2, 3)

    def step(S, inp):
        r_t, w_t, k_t, v_t, kk_s, a_t = inp
        sa = jnp.einsum('bhvk,bhk->bhv', S, -kk_s)
        S = (S * w_t[:, :, None, :] + sa[..., None] * (kk_s * a_t)[:, :, None, :]
             + v_t[..., None] * k_t[:, :, None, :])
        return S, jnp.einsum('bhvk,bhk->bhv', S, r_t)

    S0 = jnp.zeros((B, H, dh, dh), f32)
    _, y = lax.scan(step, S0, (heads(r), heads(decay), heads(k), heads(v), kk_t, heads(a)))
    y = y.transpose(1, 0, 2, 3)
    mu = jnp.mean(y, axis=-1, keepdims=True)
    var = jnp.mean((y - mu) ** 2, axis=-1, keepdims=True)
    y = ((y - mu) * lax.rsqrt(var + RWKV_GN_EPS)).reshape(B, T, RWKV_W) * ln_g + ln_b
    bonus = jnp.sum((r * k * r_k).reshape(B, T, H, dh), axis=-1, keepdims=True) * v.reshape(B, T, H, dh)
    y = y + bonus.reshape(B, T, RWKV_W)
    return y * g


def _s5(u, a_re, a_im, log_dt, b_re, b_im, c_re, c_im, d, glu_w, glu_b):
    B, T, _ = u.shape
    G, P = S5_GROUPS, S5_GROUP
    uf = u.astype(jnp.float32).reshape(B, T, G, P)
    dt = jnp.exp(log_dt)[:, None]
    mag = jnp.exp(a_re * dt)
    ang = a_im * dt
    abar_re, abar_im = mag * jnp.cos(ang), mag * jnp.sin(ang)
    den = a_re * a_re + a_im * a_im
    nr, ni = abar_re - 1.0, abar_im
    coef_re = (nr * a_re + ni * a_im) / den
    coef_im = (ni * a_re - nr * a_im) / den
    bu_re = jnp.einsum('btgp,gnp->btgn', uf, b_re)
    bu_im = jnp.einsum('btgp,gnp->btgn', uf, b_im)
    x_re = coef_re * bu_re - coef_im * bu_im
    x_im = coef_re * bu_im + coef_im * bu_re
    ar = jnp.broadcast_to(abar_re, x_re.shape)
    ai = jnp.broadcast_to(abar_im, x_re.shape)

    def combine(e1, e2):
        a1r, a1i, b1r, b1i = e1
        a2r, a2i, b2r, b2i = e2
        return (a2r * a1r - a2i * a1i, a2r * a1i + a2i * a1r,
                a2r * b1r - a2i * b1i + b2r, a2r * b1i + a2i * b1r + b2i)

    _, _, s_re, s_im = lax.associative_scan(combine, (ar, ai, x_re, x_im), axis=1)
    y = (jnp.einsum('btgn,gpn->btgp', s_re, c_re) - jnp.einsum('btgn,gpn->btgp', s_im, c_im)
         + d.reshape(G, P) * uf)
    y = jax.nn.gelu(y.reshape(B, T, S5_W))
    return y * jax.nn.sigmoid(y @ glu_w + glu_b)


def setup_inputs(seed: int = 0) -> dict:
    key = jax.random.key(seed)
    ks = iter(jax.random.split(key, 48))
    L = DEPTH
    f32 = jnp.float32

    def nrm(shape, fan_in, scale=1.0):
        return jax.random.normal(next(ks), shape, f32) * (scale * fan_in ** -0.5)

    def gain(shape, center=1.0):
        return center + 0.02 * jax.random.normal(next(ks), shape, f32)

    def unif(shape, lo, hi):
        return jax.random.uniform(next(ks), shape, f32, minval=lo, maxval=hi)

    inputs = {
        'x': jax.random.normal(next(ks), (BATCH, SEQ, D_MODEL), f32),
        'p': jax.random.normal(next(ks), (DEPTH, BATCH, SEQ, PLE_DIM), f32),
        'norm_mix_g': gain((L, D_MODEL)),
        'w_in': nrm((L, D_MODEL, N_IN), D_MODEL),
        'mlstm_conv': nrm((L, MLSTM_CONV, 2 * MLSTM_W), MLSTM_CONV),
        'mlstm_ib': -2.0 + 0.5 * jax.random.normal(next(ks), (L, MLSTM_HEADS), f32),
        'mlstm_fb': unif((L, MLSTM_HEADS), 3.0, 6.0),
        'mlstm_norm_g': gain((L, MLSTM_W)),
        'rwkv_mu': unif((L, RWKV_IN), 0.0, 1.0),
        'rwkv_w0': unif((L, RWKV_W), -6.0, -1.0),
        'rwkv_w2': nrm((L, DECAY_LORA, RWKV_W), DECAY_LORA, 0.1),
        'rwkv_a0': 0.1 * jax.random.normal(next(ks), (L, RWKV_W), f32),
        'rwkv_a2': nrm((L, AAA_LORA, RWKV_W), AAA_LORA, 0.1),
        'rwkv_g2': nrm((L, GATE_LORA, RWKV_W), GATE_LORA),
        'rwkv_kk': gain((L, RWKV_W), 0.85),
        'rwkv_ka': gain((L, RWKV_W)),
        'rwkv_rk': 0.1 * jax.random.normal(next(ks), (L, RWKV_W), f32),
        'rwkv_ln_g': gain((L, RWKV_W)),
        'rwkv_ln_b': 0.01 * jax.random.normal(next(ks), (L, RWKV_W), f32),
        's5_a_re': -0.5 + 0.01 * jax.random.normal(next(ks), (L, S5_GROUPS, S5_STATE), f32),
        's5_a_im': (jnp.pi * jnp.arange(S5_STATE, dtype=f32)
                    + 0.01 * jax.random.normal(next(ks), (L, S5_GROUPS, S5_STATE), f32)),
        's5_log_dt': unif((L, S5_GROUPS), float(np.log(1e-3)), float(np.log(1e-1))),
        's5_b_re': nrm((L, S5_GROUPS, S5_STATE, S5_GROUP), 2 * S5_GROUP),
        's5_b_im': nrm((L, S5_GROUPS, S5_STATE, S5_GROUP), 2 * S5_GROUP),
        's5_c_re': nrm((L, S5_GROUPS, S5_GROUP, S5_STATE), 2 * S5_STATE),
        's5_c_im': nrm((L, S5_GROUPS, S5_GROUP, S5_STATE), 2 * S5_STATE),
        's5_d': jax.random.normal(next(ks), (L, S5_W), f32),
        's5_glu_w': nrm((L, S5_W, S5_W), S5_W),
        's5_glu_b': 0.01 * jax.random.normal(next(ks), (L, S5_W), f32),
        'w_up_m': nrm((L, MLSTM_W, D_MODEL), MLSTM_W),
        'w_up_r': nrm((L, RWKV_W, D_MODEL), RWKV_W),
        'w_up_s': nrm((L, S5_W, D_MODEL), S5_W),
        'w_out': nrm((L, D_MODEL, D_MODEL), D_MODEL),
        'norm_ffn_g': gain((L, D_MODEL)),
        'ffn_w_gate': nrm((L, D_MODEL, FFN_HIDDEN), D_MODEL),
        'ffn_w_up': nrm((L, D_MODEL, FFN_HIDDEN), D_MODEL),
        'ffn_w_down': nrm((L, FFN_HIDDEN, D_MODEL), FFN_HIDDEN),
        'norm_ple_g': gain((L, D_MODEL)),
        'ple_w_gate': nrm((L, D_MODEL, D_MODEL), D_MODEL),
        'ple_w_proj': nrm((L, PLE_DIM, D_MODEL), PLE_DIM),
        'final_norm_g': gain((D_MODEL,)),
    }
    return inputs


def reference(x, p, norm_mix_g, w_in, mlstm_conv, mlstm_ib, mlstm_fb, mlstm_norm_g,
              rwkv_mu, rwkv_w0, rwkv_w2, rwkv_a0, rwkv_a2, rwkv_g2, rwkv_kk, rwkv_ka, rwkv_rk,
              rwkv_ln_g, rwkv_ln_b, s5_a_re, s5_a_im, s5_log_dt, s5_b_re, s5_b_im, s5_c_re,
              s5_c_im, s5_d, s5_glu_w, s5_glu_b, w_up_m, w_up_r, w_up_s, w_out, norm_ffn_g,
              ffn_w_gate, ffn_w_up, ffn_w_down, norm_ple_g, ple_w_gate, ple_w_proj, final_norm_g):
    h = x
    dt = x.dtype
    for i in range(DEPTH):
        xn = _rmsnorm(h, norm_mix_g[i])
        z = xn @ w_in[i]
        zm, zr, zs, zg = jnp.split(z, IN_SPLIT, axis=-1)

        qk = jax.nn.silu(_causal_dwconv(zm[..., :2 * MLSTM_W], mlstm_conv[i]))
        q, k = jnp.split(qk, 2, axis=-1)
        v = zm[..., 2 * MLSTM_W:3 * MLSTM_W]
        o = zm[..., 3 * MLSTM_W:4 * MLSTM_W]
        ig = zm[..., 4 * MLSTM_W:4 * MLSTM_W + MLSTM_HEADS] + mlstm_ib[i]
        fg = zm[..., 4 * MLSTM_W + MLSTM_HEADS:] + mlstm_fb[i]
        y_m = _mlstm(q, k, v, o, ig, fg, mlstm_norm_g[i])

        zr = _token_shift(zr, rwkv_mu[i])
        r, kr, vr, wl, al, gl = jnp.split(zr, RWKV_SPLIT, axis=-1)
        y_r = _rwkv7(r, kr, vr, wl, al, gl, rwkv_w0[i], rwkv_w2[i], rwkv_a0[i], rwkv_a2[i],
                     rwkv_g2[i], rwkv_kk[i], rwkv_ka[i], rwkv_rk[i], rwkv_ln_g[i], rwkv_ln_b[i])

        y_s = _s5(zs, s5_a_re[i], s5_a_im[i], s5_log_dt[i], s5_b_re[i], s5_b_im[i],
                  s5_c_re[i], s5_c_im[i], s5_d[i], s5_glu_w[i], s5_glu_b[i])

        g_m, g_r, g_s = jnp.split(jax.nn.sigmoid(zg), N_BRANCH, axis=-1)
        mixed = (g_m * (y_m.astype(dt) @ w_up_m[i]) + g_r * (y_r.astype(dt) @ w_up_r[i])
                 + g_s * (y_s.astype(dt) @ w_up_s[i]))
        h = h + mixed @ w_out[i]

        hn = _rmsnorm(h, norm_ffn_g[i])
        h = h + (jax.nn.silu(hn @ ffn_w_gate[i]) * (hn @ ffn_w_up[i])) @ ffn_w_down[i]

        hp = _rmsnorm(h, norm_ple_g[i])
        h = h + (p[i] @ ple_w_proj[i]) * jax.nn.sigmoid(hp @ ple_w_gate[i])
    return _rmsnorm(h, final_norm_g)
```

```python
import numpy as np
import concourse.bass as bass
import concourse.mybir as mybir
from concourse.bass_utils import run_bass_kernel_spmd

F32 = mybir.dt.float32
BF16 = mybir.dt.bfloat16
AF = mybir.ActivationFunctionType
ALU = mybir.AluOpType
AX = mybir.AxisListType


def _box(ap):
    t = ap.tensor
    shp = tuple(t.shape)
    space = str(ap.space)
    if 'DRAM' in space.upper() or 'HBM' in space.upper():
        F = 1 << 62
    else:
        F = 1
        for s in shp[1:]:
            F *= int(s)
    off = int(ap.offset)
    p0 = off // F
    f0 = off % F
    ps = 0
    fs = 0
    for step, cnt in ap.ap:
        step = int(step)
        cnt = int(cnt)
        if cnt <= 1 or step == 0:
            continue
        if step % F == 0:
            ps += (cnt - 1) * (step // F)
        else:
            fs += (cnt - 1) * step
    return (t.name, p0, p0 + ps, f0, f0 + fs)


def _ovl(a, b):
    return not (a[2] < b[1] or b[2] < a[1] or a[4] < b[3] or b[4] < a[3])


def _cov(a, b):
    return a[1] <= b[1] and a[2] >= b[2] and a[3] <= b[3] and a[4] >= b[4]


class Prog:
    def __init__(self, nc, n_dma_sems=24):
        self.nc = nc
        self.ops = []
        self.eng = dict(pe=nc.tensor, dve=nc.vector, act=nc.scalar, pool=nc.gpsimd, sp=nc.sync)
        self.n_dma_sems = n_dma_sems
        self.acc = {}

    def op(self, eng, fn, reads=(), writes=(), dma=False, pe_acc=False):
        i = len(self.ops)
        deps = set()
        rb = [_box(a) for a in reads]
        wb = [_box(a) for a in writes]
        for b in rb:
            for (ob, oi, ow) in self.acc.get(b[0], ()):
                if ow and _ovl(b, ob):
                    deps.add(oi)
        for b in wb:
            for (ob, oi, ow) in self.acc.get(b[0], ()):
                if _ovl(b, ob):
                    deps.add(oi)
        deps.discard(i)
        for b in wb:
            lst = self.acc.setdefault(b[0], [])
            lst[:] = [e for e in lst if not _cov(b, e[0])]
            lst.append((b, i, True))
        for b in rb:
            lst = self.acc.setdefault(b[0], [])
            lst[:] = [e for e in lst if not (not e[2] and e[0] == b and e[1] < i and self.ops[e[1]]['eng'] == eng and not self.ops[e[1]]['dma'] and not dma)]
            lst.append((b, i, False))
        if pe_acc:
            deps = {d for d in deps if not (self.ops[d]['eng'] == 'pe' and self.ops[d]['pe'])}
        self.ops.append(dict(eng=eng, fn=fn, deps=deps, dma=dma, pe=(eng == 'pe')))
        return i

    def emit(self):
        nc = self.nc
        ops = self.ops
        needed = [False] * len(ops)
        for o in ops:
            for d in o['deps']:
                needed[d] = True
        esem = {e: nc.alloc_semaphore('sem_' + e) for e in self.eng}
        dsem = [nc.alloc_semaphore('dsem%d' % k) for k in range(self.n_dma_sems)]
        semid = {}
        sems = []
        for e in self.eng:
            semid[('e', e)] = len(sems)
            sems.append(esem[e])
        for k in range(self.n_dma_sems):
            semid[('d', k)] = len(sems)
            sems.append(dsem[k])
        ns = len(sems)
        ecount = {e: 0 for e in self.eng}
        dcount = [0] * self.n_dma_sems
        dnext = 0
        seen = {e: [0] * ns for e in self.eng}
        sig = [None] * len(ops)
        know = [None] * len(ops)
        nwait = 0
        for i, o in enumerate(ops):
            e = o['eng']
            E = self.eng[e]
            sv = seen[e]
            req = {}
            for d in o['deps']:
                si, val = sig[d]
                if sv[si] >= val:
                    continue
                if req.get(si, 0) < val:
                    req[si] = val
            dk = None
            if o['dma']:
                dk = dnext
                dnext = (dnext + 1) % self.n_dma_sems
                si = semid[('d', dk)]
                if dcount[dk] > sv[si]:
                    req[si] = max(req.get(si, 0), dcount[dk])
            for d in o['deps']:
                si, val = sig[d]
                if si in req and req[si] <= val:
                    kd = know[d]
                    if kd is not None:
                        for sj in list(req.keys()):
                            if sj != si and kd[sj] >= req[sj]:
                                del req[sj]
            for si, val in req.items():
                E.wait_ge(sems[si], val)
                nwait += 1
                if sv[si] < val:
                    sv[si] = val
            for d in o['deps']:
                kd = know[d]
                if kd is not None:
                    for sj in range(ns):
                        if kd[sj] > sv[sj]:
                            sv[sj] = kd[sj]
            ins = o['fn'](E)
            if o['dma']:
                dcount[dk] += 16
                ins.then_inc(dsem[dk], 16)
                si = semid[('d', dk)]
                sig[i] = (si, dcount[dk])
                know[i] = list(sv)
            else:
                si = semid[('e', e)]
                if needed[i]:
                    ecount[e] += 1
                    ins.then_inc(esem[e], 1)
                    sig[i] = (si, ecount[e])
                    k = list(sv)
                    k[si] = ecount[e]
                    know[i] = k
                else:
                    sig[i] = (si, ecount[e] + 1)
                    know[i] = None
        self.final = (sems, semid, dcount, ecount)
        self.nwait = nwait
        return nwait

    def finish(self, eng='sp'):
        pass
EPS = 1e-6


class Ctx:
    def __init__(self, nc, n_dma_sems=24):
        self.nc = nc
        self.P = Prog(nc, n_dma_sems)
        self.dummy = nc.alloc_semaphore("dummy_fin")
        self.rr = 0
        self.psb = [nc.alloc_psum_tensor("psb%d" % i, [128, 512], F32) for i in range(8)]
        self.psi = 0

    def sb(self, name, shape, dt=F32):
        return self.nc.alloc_sbuf_tensor("s_" + name, list(shape), dt)

    def ps(self, lo=0, hi=8):
        b = self.psb[lo + (self.psi % (hi - lo))]
        self.psi += 1
        return b

    def dma(self, out, in_, q='sp'):
        self.P.op(q, lambda E: E.dma_start(out=out, in_=in_), reads=[in_], writes=[out], dma=True)

    def mm(self, out, lhsT, rhs, start=True, stop=True):
        self.P.op('pe', lambda E: E.matmul(out, lhsT, rhs, start=start, stop=stop),
                  reads=[lhsT, rhs], writes=[out], pe_acc=not start)

    def tr(self, out, in_, ident):
        self.P.op('pe', lambda E: E.transpose(out, in_, ident), reads=[in_, ident], writes=[out])

    def act(self, out, in_, func, bias=None, scale=1.0, accum=None, eng='act'):
        rd = [in_]
        kw = {}
        if bias is not None:
            kw['bias'] = bias
            if not isinstance(bias, (int, float)):
                rd.append(bias)
        if not isinstance(scale, (int, float)):
            rd.append(scale)
        wr = [out]
        if accum is not None:
            kw['accum_out'] = accum
            wr.append(accum)
        self.P.op('act', lambda E: E.activation(out, in_, func, scale=scale, **kw), reads=rd, writes=wr)

    def tt(self, eng, out, a, b, op):
        self.P.op(eng, lambda E: E.tensor_tensor(out, a, b, op), reads=[a, b], writes=[out])

    def ts(self, eng, out, a, s1, op0, s2=None, op1=None, accum=None):
        rd = [a] + [s for s in (s1, s2) if s is not None and not isinstance(s, (int, float))]
        wr = [out] + ([accum] if accum is not None else [])
        kw = {}
        if op1 is not None:
            kw['op1'] = op1
        if accum is not None:
            kw['accum_out'] = accum
        self.P.op(eng, lambda E: E.tensor_scalar(out, a, s1, s2, op0, **kw), reads=rd, writes=wr)

    def stt(self, eng, out, in0, scalar, in1, op0, op1):
        rd = [in0, in1] + ([scalar] if not isinstance(scalar, (int, float)) else [])
        eng = 'dve'
        self.P.op(eng, lambda E: E.scalar_tensor_tensor(out, in0, scalar, in1, op0, op1), reads=rd, writes=[out])

    def cp(self, eng, out, in_):
        if eng == 'act':
            self.P.op('act', lambda E: E.copy(out, in_), reads=[in_], writes=[out])
        else:
            self.P.op(eng, lambda E: E.tensor_copy(out, in_), reads=[in_], writes=[out])

    def memset(self, eng, out, val):
        self.P.op(eng, lambda E: E.memset(out, val), writes=[out])

    def recip(self, out, in_):
        self.P.op('dve', lambda E: E.reciprocal(out, in_), reads=[in_], writes=[out])

    def scan(self, out, d0, d1, init, op0=None, op1=None, eng='dve'):
        op0 = op0 or ALU.mult
        op1 = op1 or ALU.add
        rd = [d0, d1] + ([init] if not isinstance(init, (int, float)) else [])
        self.P.op(eng, lambda E: E.tensor_tensor_scan(out, d0, d1, init, op0, op1), reads=rd, writes=[out])

    def asel(self, out, in_, pattern, cmp, fill, base=0, cm=0):
        self.P.op('pool', lambda E: E.affine_select(out, in_, pattern, cmp, fill, base=base, channel_multiplier=cm),
                  reads=[in_], writes=[out])

    def finish(self, outs):
        d = self.dummy
        self.P.op('sp', lambda E: E.sem_inc(d, 1), reads=list(outs))
        return self.P.emit()

    def evac(self, out, in_):
        self.rr += 1
        if self.rr % 2:
            self.cp('dve', out, in_)
        else:
            self.cp('act', out, in_)


def make_consts(C):
    C.ones = C.sb("c_ones", [128, 128])
    C.memset('pool', C.ones[:], 1.0)
    C.ident = C.sb("c_ident", [128, 128])
    C.memset('pool', C.ident[:], 1.0)
    C.asel(C.ident[:], C.ident[:], [[1, 128]], ALU.is_equal, 0.0, base=0, cm=-1)
    C.cb = C.sb("c_cb", [128, 4])
    C.memset('pool', C.cb[:, 0:1], 1.0)
    C.memset('pool', C.cb[:, 1:2], 1.5707963267948966)
    C.memset('pool', C.cb[:, 2:3], EPS)
    C.identb = C.sb("c_identb", [128, 128], BF16)
    C.cp('pool', C.identb[:], C.ident[:])


def rmsnorm_fm(C, hT, KT, T, g_sb, xn, sq_scr, rstd):
    nch = (T + 511) // 512
    banks = [C.psb[6], C.psb[7]]
    for kt in range(KT):
        s = sq_scr[:, kt % 2, :]
        C.act(s, hT[:, kt, :], AF.Square)
        for ch in range(nch):
            n = min(512, T - ch * 512)
            C.mm(banks[ch][:, :n], C.ones[:], s[:, ch * 512:ch * 512 + n], start=(kt == 0), stop=(kt == KT - 1))
    for ch in range(nch):
        n = min(512, T - ch * 512)
        C.act(rstd[:, ch * 512:ch * 512 + n], banks[ch][:, :n], AF.Sqrt, bias=C.epsb[:, 0:1], scale=1.0 / (KT * 128))
    C.recip(rstd[:, :T], rstd[:, :T])
    for kt in range(KT):
        C.stt('dve', xn[:, kt, :], hT[:, kt, :], g_sb[:, kt:kt + 1], rstd[:, :T], ALU.mult, ALU.mult)


def dense_fm(C, w2d, K, col0, ncols, rhs, T, epi, wbufs, gcols=256, tag=""):
    KT = (K + 127) // 128
    wv = w2d.rearrange("(kt p) n -> p kt n", p=128) if K % 128 == 0 else None
    c = col0
    gi = 0
    while c < col0 + ncols:
        gw = min(gcols, col0 + ncols - c)
        wb = wbufs[C.wrr % len(wbufs)]
        C.wrr += 1
        wbv = wb[:, 0:KT * gw].rearrange("p (kt n) -> p kt n", kt=KT)
        if wv is not None:
            C.dma(wbv, wv[:, :, c:c + gw], q='pool')
        else:
            for kt in range(KT):
                kk = min(128, K - kt * 128)
                C.dma(wbv[:kk, kt, :], w2d[kt * 128:kt * 128 + kk, c:c + gw], q='pool')
        for cc in range(0, gw, 128):
            cw = min(128, gw - cc)
            for t0 in range(0, T, 512):
                n = min(512, T - t0)
                ps = C.ps(0, 4)
                for kt in range(KT):
                    kk = min(128, K - kt * 128)
                    C.mm(ps[:cw, :n], wbv[:kk, kt, cc:cc + cw], rhs(kt)[:kk, t0:t0 + n], start=(kt == 0), stop=(kt == KT - 1))
                epi(c + cc, cw, ps, t0, n)
        c += gw
        gi += 1


def build_A(T=1024, D=2048, NIN=12744):
    nc = bass.Bass("TRN2", target_bir_lowering=False)
    KT = D // 128
    hT_d = nc.dram_tensor("hT", [D, T], F32, kind="ExternalInput").ap()
    g_d = nc.dram_tensor("g", [128, KT], F32, kind="ExternalInput").ap()
    w_d = nc.dram_tensor("w", [D, NIN], F32, kind="ExternalInput").ap()
    z_d = nc.dram_tensor("zT", [NIN, T], F32, kind="ExternalOutput").ap()
    C = Ctx(nc)
    C.wrr = 0
    make_consts(C)
    C.epsb = C.sb("epsb", [128, 1])
    C.memset('pool', C.epsb[:], EPS)
    hT = C.sb("hT_sb", [128, KT, T])
    g_sb = C.sb("g_sb", [128, KT])
    xn = C.sb("xn", [128, KT, T], BF16)
    sq = C.sb("sq", [128, 2, T])
    rstd = C.sb("rstd", [128, T])
    wbufs = [C.sb("wb%d" % i, [128, KT * 256], BF16) for i in range(3)]
    zs = [C.sb("zs%d" % i, [128, 512]) for i in range(4)]
    C.dma(g_sb[:], g_d)
    hv = hT_d.rearrange("(kt p) t -> p kt t", p=128)
    for kt in range(KT):
        C.dma(hT[:, kt, :], hv[:, kt, :])
    rmsnorm_fm(C, hT, KT, T, g_sb, xn, sq, rstd)
    st = [0]

    def epi(c0, cw, ps, t0, n):
        z = zs[st[0] % 4]
        st[0] += 1
        C.evac(z[:cw, :n], ps[:cw, :n])
        C.dma(z_d[c0:c0 + cw, t0:t0 + n], z[:cw, :n])
    dense_fm(C, w_d, D, 0, NIN, lambda kt: xn[:, kt, :], T, epi, wbufs)
    nw = C.finish([z_d])
    return nc, nw

TWO_PI = 6.283185307179586


def build_S5(C, T8, zs_d, s5p_d, s5b_d, s5c_d, s5d_d, ys_d):
    sb = C.sb
    TB = min(512, T8)
    NL = TB.bit_length() - 1
    pr = sb("s5p", [128, 6]); C.dma(pr[:], s5p_d)
    bw32 = sb("s5b32", [64, 512]); C.dma(bw32[:], s5b_d)
    cw32 = sb("s5c32", [128, 256]); C.dma(cw32[:], s5c_d)
    dsk = sb("s5d", [64, 1]); C.dma(dsk[:], s5d_d)
    bw = sb("s5bw", [64, 512], BF16); C.cp('pool', bw[:], bw32[:])
    cw = sb("s5cw", [128, 256], BF16)
    C.cp('pool', cw[:, 0:128], cw32[:, 0:128])
    C.ts('dve', cw[:, 128:256], cw32[:, 128:256], -1.0, ALU.mult)
    v = sb("s5v", [128, 40])
    A_RE, A_IM, LDT = pr[:, 0:2], pr[:, 2:4], pr[:, 4:6]
    dt = v[:, 0:2]; C.act(dt, LDT, AF.Exp)
    mag = v[:, 2:4]; C.tt('dve', mag, A_RE, dt, ALU.mult); C.act(mag, mag, AF.Exp)
    ang = v[:, 4:6]; C.tt('dve', ang, A_IM, dt, ALU.mult)
    sn = v[:, 6:8]; cs = v[:, 8:10]
    th = v[:, 28:30]; C.ts('dve', th, ang, 1.0 / 16, ALU.mult)
    C.act(sn, th, AF.Sin)
    C.act(cs, th, AF.Sin, bias=C.cb[:, 1:2])
    ar = v[:, 10:12]; ai = v[:, 12:14]
    q1 = v[:, 30:32]; q2 = v[:, 32:34]
    for _ in range(4):
        C.tt('dve', q1, cs, cs, ALU.mult); C.tt('dve', q2, sn, sn, ALU.mult)
        C.tt('dve', q2, q1, q2, ALU.subtract)
        C.tt('dve', q1, cs, sn, ALU.mult)
        C.ts('dve', sn, q1, 2.0, ALU.mult)
        C.cp('dve', cs, q2)
    C.tt('dve', ar, mag, cs, ALU.mult)
    C.tt('dve', ai, mag, sn, ALU.mult)
    den = v[:, 14:16]; t1 = v[:, 16:18]; t2 = v[:, 18:20]
    C.tt('dve', den, A_RE, A_RE, ALU.mult); C.tt('dve', t1, A_IM, A_IM, ALU.mult); C.tt('dve', den, den, t1, ALU.add)
    C.recip(den, den)
    nr = v[:, 20:22]; C.ts('dve', nr, ar, -1.0, ALU.add)
    cre = v[:, 22:24]; cim = v[:, 24:26]
    C.tt('dve', t1, nr, A_RE, ALU.mult); C.tt('dve', t2, ai, A_IM, ALU.mult); C.tt('dve', t1, t1, t2, ALU.add); C.tt('dve', cre, t1, den, ALU.mult)
    C.tt('dve', t1, ai, A_RE, ALU.mult); C.tt('dve', t2, nr, A_IM, ALU.mult); C.tt('dve', t1, t1, t2, ALU.subtract); C.tt('dve', cim, t1, den, ALU.mult)
    ncim = v[:, 26:28]; C.ts('dve', ncim, cim, -1.0, ALU.mult)
    pwr = sb("s5pwr", [128, 2, NL + 1]); pwi = sb("s5pwi", [128, 2, NL + 1]); pwn = sb("s5pwn", [128, 2, NL + 1])
    C.cp('dve', pwr[:, :, 0], ar); C.cp('dve', pwi[:, :, 0], ai)
    for k in range(NL):
        C.tt('dve', t1, pwr[:, :, k], pwr[:, :, k], ALU.mult)
        C.tt('dve', t2, pwi[:, :, k], pwi[:, :, k], ALU.mult)
        C.tt('dve', pwr[:, :, k + 1], t1, t2, ALU.subtract)
        C.tt('dve', t1, pwr[:, :, k], pwi[:, :, k], ALU.mult)
        C.ts('dve', pwi[:, :, k + 1], t1, 2.0, ALU.mult)
    C.ts('dve', pwn[:], pwi[:], -1.0, ALU.mult)
    u32 = [sb("s5u32_%d" % i, [64, TB]) for i in range(2)]
    ub = [sb("s5ub_%d" % i, [64, TB], BF16) for i in range(2)]
    xr = [sb("s5xr%d" % i, [128, TB]) for i in range(2)]
    xi = [sb("s5xi%d" % i, [128, TB]) for i in range(2)]
    tm = [sb("s5tm%d" % i, [128, TB]) for i in range(2)]
    sbf = [[sb("s5sb%d_%d" % (j, c), [128, TB], BF16) for c in range(2)] for j in range(2)]
    carry = sb("s5carry", [128, 4]); C.memset('pool', carry[:], 0.0)
    cz = sb("s5cz", [128, 4])
    yo = [sb("s5yo%d" % i, [64, 512]) for i in range(2)]
    nblk = T8 // TB
    for b in range(nblk):
        t0 = b * TB
        uu = u32[b % 2]; ubb = ub[b % 2]
        C.dma(uu[:], zs_d[:, t0:t0 + TB])
        C.dma(ubb[:], zs_d[:, t0:t0 + TB], q='pool')
        for j in range(2):
            for c0 in range(0, TB, 512):
                n = min(512, TB - c0)
                p1 = C.ps(0, 4); p2 = C.ps(0, 4)
                C.mm(p1[:, :n], bw[:, j * 128:(j + 1) * 128], ubb[:, c0:c0 + n])
                C.mm(p2[:, :n], bw[:, 256 + j * 128:256 + (j + 1) * 128], ubb[:, c0:c0 + n])
                C.ts('dve', tm[0][:, c0:c0 + n], p2[:, :n], ncim[:, j:j + 1], ALU.mult)
                C.stt('dve', xr[0][:, c0:c0 + n], p1[:, :n], cre[:, j:j + 1], tm[0][:, c0:c0 + n], ALU.mult, ALU.add)
                C.ts('dve', tm[1][:, c0:c0 + n], p1[:, :n], cim[:, j:j + 1], ALU.mult)
                C.stt('dve', xi[0][:, c0:c0 + n], p2[:, :n], cre[:, j:j + 1], tm[1][:, c0:c0 + n], ALU.mult, ALU.add)
            cr = carry[:, j:j + 1]; ci = carry[:, 2 + j:3 + j]
            C.tt('dve', cz[:, 0:1], cr, ar[:, j:j + 1], ALU.mult)
            C.tt('dve', cz[:, 1:2], ci, ai[:, j:j + 1], ALU.mult)
            C.tt('dve', cz[:, 0:1], cz[:, 0:1], cz[:, 1:2], ALU.subtract)
            C.tt('dve', xr[0][:, 0:1], xr[0][:, 0:1], cz[:, 0:1], ALU.add)
            C.tt('dve', cz[:, 2:3], cr, ai[:, j:j + 1], ALU.mult)
            C.tt('dve', cz[:, 3:4], ci, ar[:, j:j + 1], ALU.mult)
            C.tt('dve', cz[:, 2:3], cz[:, 2:3], cz[:, 3:4], ALU.add)
            C.tt('dve', xi[0][:, 0:1], xi[0][:, 0:1], cz[:, 2:3], ALU.add)
            cur = 0
            for k in range(NL):
                d = 1 << k
                a, bb = cur, 1 - cur
                n = TB - d
                C.stt('dve', tm[0][:, 0:n], xi[a][:, 0:n], pwn[:, j, k:k + 1], xr[a][:, d:TB], ALU.mult, ALU.add)
                C.stt('dve', xr[bb][:, d:TB], xr[a][:, 0:n], pwr[:, j, k:k + 1], tm[0][:, 0:n], ALU.mult, ALU.add)
                C.stt('dve', tm[1][:, 0:n], xr[a][:, 0:n], pwi[:, j, k:k + 1], xi[a][:, d:TB], ALU.mult, ALU.add)
                C.stt('dve', xi[bb][:, d:TB], xi[a][:, 0:n], pwr[:, j, k:k + 1], tm[1][:, 0:n], ALU.mult, ALU.add)
                C.cp('pool', xr[bb][:, 0:d], xr[a][:, 0:d])
                C.cp('pool', xi[bb][:, 0:d], xi[a][:, 0:d])
                cur = bb
            C.cp('pool', carry[:, j:j + 1], xr[cur][:, TB - 1:TB])
            C.cp('pool', carry[:, 2 + j:3 + j], xi[cur][:, TB - 1:TB])
            C.cp('act', sbf[j][0][:], xr[cur][:])
            C.cp('act', sbf[j][1][:], xi[cur][:])
        for c0 in range(0, TB, 512):
            n = min(512, TB - c0)
            p = C.ps(0, 4)
            k = 0
            for j in range(2):
                for c in range(2):
                    C.mm(p[0:64, :n], cw[:, c * 128 + j * 64:c * 128 + (j + 1) * 64], sbf[j][c][:, c0:c0 + n], start=(k == 0), stop=(k == 3))
                    k += 1
            y = yo[(c0 // 512) % 2]
            C.stt('dve', y[:, :n], uu[:, c0:c0 + n], dsk[:, 0:1], p[0:64, :n], ALU.mult, ALU.add)
            C.act(y[:, :n], y[:, :n], AF.Gelu_apprx_tanh)
            C.dma(ys_d[:, t0 + c0:t0 + c0 + n], y[:, :n])

GN_EPS = 64e-5


def build_RWKV(C, T8, zr_d, rp_d, rw2_d, rg2_d, rln_d, yr_d):
    sb = C.sb
    TB = min(256, T8)
    NC = TB // 64
    rp = sb("rp", [128, 16]); C.dma(rp[:], rp_d)
    rw2 = sb("rw2", [96, 128]); C.dma(rw2[:], rw2_d)
    rg2 = sb("rg2", [128, 128]); C.dma(rg2[:], rg2_d)
    rln = sb("rln", [64, 128]); C.dma(rln[:], rln_d)
    m_ui = sb("m_ui", [64, NC, 64]); m_us = sb("m_us", [64, NC, 64]); m_ls = sb("m_ls", [64, NC, 64])
    for mt, op, st, cm in ((m_ui, ALU.is_ge, 1, -1), (m_us, ALU.is_gt, 1, -1), (m_ls, ALU.is_gt, -1, 1)):
        C.memset('pool', mt[:], 1.0)
        C.asel(mt[:], mt[:], [[0, NC], [st, 64]], op, 0.0, base=0, cm=cm)
    identc = sb("identc", [64, NC, 64])
    C.memset('pool', identc[:], 1.0)
    C.asel(identc[:], identc[:], [[0, NC], [1, 64]], ALU.is_equal, 0.0, base=0, cm=-1)
    epsg = sb("epsg", [64, 1]); C.memset('pool', epsg[:], GN_EPS)
    M = sb("rM", [64, 64]); C.memset('pool', M[:], 0.0)
    pieces = [(0, 64), (64, 64), (128, 64), (192, 96), (288, 96), (384, 128), (512, 128)]
    raw = [sb("rraw%d" % i, [n, TB + 1]) for i, (o, n) in enumerate(pieces)]
    prevcol = [sb("rprev%d" % i, [n, 1]) for i, (o, n) in enumerate(pieces)]
    sh = [sb("rsh%d" % i, [n, TB]) for i, (o, n) in enumerate(pieces)]
    tmpa = sb("rtmpa", [128, TB])
    f = lambda name, p=64: sb(name, [p, TB])
    lw = f("r_lw"); aa = f("r_a"); kk = f("r_kk"); km = f("r_km"); cum = f("r_cum")
    pp = f("r_p"); pinv = f("r_pinv"); pm1 = f("r_pm1")
    rt = f("r_rt"); kt_ = f("r_kt"); at = f("r_at"); bt = f("r_bt"); t64 = f("r_t64"); t64b = f("r_t64b")
    sgl = sb("r_sgl", [128, 2, TB])
    ones64 = sb("ones64c", [64, TB]); C.memset('pool', ones64[:], 1.0)
    vtok = sb("r_vtok", [64, NC, 64]); atok = sb("r_atok", [64, NC, 64]); ktok = sb("r_ktok", [64, NC, 64])
    gN = sb("r_N", [64, NC, 64]); gNT = sb("r_NT", [64, NC, 64]); gN2 = sb("r_N2", [64, NC, 64]); gNT2 = sb("r_NT2", [64, NC, 64])
    WT = sb("r_WT", [64, NC, 64]); WT2 = sb("r_WT2", [64, NC, 64])
    AakT = sb("r_AakT", [64, NC, 64]); AraT = sb("r_AraT", [64, NC, 64]); ArkT = sb("r_ArkT", [64, NC, 64])
    gtok = sb("r_gtok", [64, NC, 64]); bsum = sb("r_bsum", [64, NC])
    ytok = sb("r_ytok", [64, NC, 64]); X0 = sb("r_X0", [64, 64]); U = sb("r_U", [64, 64]); MpL = sb("r_MpL", [64, 64])
    st8 = sb("r_st8", [64, 4 * NC]); yo = sb("r_yo", [64, NC, 64])
    nblk = T8 // TB
    for b in range(nblk):
        t0 = b * TB
        for i, (o, n) in enumerate(pieces):
            if b == 0:
                C.memset('pool', raw[i][:, 0:1], 0.0)
            else:
                C.cp('pool', raw[i][:, 0:1], prevcol[i][:])
            C.dma(raw[i][:, 1:TB + 1], zr_d[o:o + n, t0:t0 + TB])
            C.cp('pool', prevcol[i][:], raw[i][:, TB:TB + 1])
            C.tt('pool', tmpa[:n, :], raw[i][:, 0:TB], raw[i][:, 1:TB + 1], ALU.subtract)
            C.stt('dve', sh[i][:], tmpa[:n, :], rp[:n, i:i + 1], raw[i][:, 1:TB + 1], ALU.mult, ALU.add)
        r_, k_, v_, wl_, al_, gl0, gl1 = sh
        C.act(wl_[:], wl_[:], AF.Tanh)
        ps = C.ps(0, 4)
        C.mm(ps[0:64, :TB], rw2[:, 0:64], wl_[:])
        C.act(lw[:], ps[0:64, :TB], AF.Sigmoid, bias=rp[0:64, 7:8])
        C.ts('dve', lw[:], lw[:], -0.6065306597126334, ALU.mult)
        ps = C.ps(0, 4)
        C.mm(ps[0:64, :TB], rw2[:, 64:128], al_[:])
        C.act(aa[:], ps[0:64, :TB], AF.Sigmoid, bias=rp[0:64, 8:9])
        C.act(sgl[:, 0, :], gl0[:], AF.Sigmoid)
        C.act(sgl[:, 1, :], gl1[:], AF.Sigmoid)
        psg = C.ps(4, 8)
        for c in range(NC):
            for h2 in range(2):
                C.mm(psg[0:64, c * 64:(c + 1) * 64], sgl[:, h2, c * 64:(c + 1) * 64], rg2[:, h2 * 64:(h2 + 1) * 64], start=(h2 == 0), stop=(h2 == 1))
        C.cp('act', gtok[:], psg[0:64, 0:NC * 64].rearrange("p (c v) -> p c v", c=NC))
        C.ts('dve', kk[:], k_[:], rp[0:64, 9:10], ALU.mult)
        C.act(t64[:], kk[:], AF.Square)
        ps = C.ps(0, 4)
        C.mm(ps[0:64, :TB], C.ones[0:64, 0:64], t64[:])
        C.act(t64[:], ps[0:64, :TB], AF.Sqrt)
        C.ts('dve', t64[:], t64[:], 1e-12, ALU.max)
        C.recip(t64[:], t64[:])
        C.tt('dve', kk[:], kk[:], t64[:], ALU.mult)
        C.ts('dve', t64[:], aa[:], -1.0, ALU.add, rp[0:64, 10:11], ALU.mult)
        C.ts('dve', t64[:], t64[:], 1.0, ALU.add)
        C.tt('dve', km[:], k_[:], t64[:], ALU.mult)
        for c in range(NC):
            C.scan(cum[:, c * 64:(c + 1) * 64], ones64[:, 0:64], lw[:, c * 64:(c + 1) * 64], 0.0)
        C.act(pp[:], cum[:], AF.Exp)
        C.act(pinv[:], cum[:], AF.Exp, scale=-1.0)
        C.tt('pool', t64[:], cum[:], lw[:], ALU.subtract)
        C.act(pm1[:], t64[:], AF.Exp)
        C.tt('dve', rt[:], r_[:], pp[:], ALU.mult)
        C.tt('dve', kt_[:], km[:], pinv[:], ALU.mult)
        C.tt('dve', t64[:], kk[:], aa[:], ALU.mult)
        C.stt('dve', at[:], t64[:], -1.0, pinv[:], ALU.mult, ALU.mult)
        C.tt('dve', bt[:], kk[:], pm1[:], ALU.mult)
        C.stt('dve', t64b[:], r_[:], rp[0:64, 11:12], km[:], ALU.mult, ALU.mult)
        psb_ = C.ps(4, 8)
        for c in range(NC):
            C.mm(psb_[0:64, c:c + 1], t64b[:, c * 64:(c + 1) * 64], C.ones[0:64, 0:1])
        C.cp('dve', bsum[:], psb_[0:64, 0:NC])
        for src, dst in ((v_, vtok), (at, atok), (kt_, ktok)):
            pt = C.ps(4, 8)
            for c in range(NC):
                C.tr(pt[0:64, c * 64:(c + 1) * 64], src[:, c * 64:(c + 1) * 64], C.ident[0:64, 0:64])
            C.evac(dst[:], pt[0:64, 0:NC * 64].rearrange("p (c v) -> p c v", c=NC))
        def gram(dst, lhs, rhs, mask):
            pg = C.ps(4, 8)
            for c in range(NC):
                C.mm(pg[0:64, c * 64:(c + 1) * 64], lhs[:, c * 64:(c + 1) * 64], rhs[:, c * 64:(c + 1) * 64])
            C.tt('dve', dst[:], pg[0:64, 0:NC * 64].rearrange("p (c v) -> p c v", c=NC), mask[:], ALU.mult)
        gram(gN, bt, at, m_ls)
        gram(gNT, at, bt, m_us)
        gram(AakT, kt_, bt, m_us)
        gram(AraT, at, rt, m_ui)
        gram(ArkT, kt_, rt, m_ui)
        C.tt('dve', WT[:], gNT[:], identc[:], ALU.add)
        Nk, NkT, Nn, NnT = gN, gNT, gN2, gNT2
        Wc, Wn = WT, WT2
        for lev in range(5):
            p1 = C.ps(4, 8); p2 = C.ps(4, 8)
            for c in range(NC):
                C.mm(p1[0:64, c * 64:(c + 1) * 64], NkT[:, c, :], Nk[:, c, :])
                C.mm(p2[0:64, c * 64:(c + 1) * 64], Nk[:, c, :], NkT[:, c, :])
            C.cp('dve', Nn[:], p1[0:64, 0:NC * 64].rearrange("p (c v) -> p c v", c=NC))
            C.cp('act', NnT[:], p2[0:64, 0:NC * 64].rearrange("p (c v) -> p c v", c=NC))
            p3 = C.ps(4, 8)
            for c in range(NC):
                C.mm(p3[0:64, c * 64:(c + 1) * 64], Nn[:, c, :], Wc[:, c, :])
            C.tt('dve', Wn[:], p3[0:64, 0:NC * 64].rearrange("p (c v) -> p c v", c=NC), Wc[:], ALU.add)
            Nk, NkT, Nn, NnT = Nn, NnT, Nk, NkT
            Wc, Wn = Wn, Wc
        for c in range(NC):
            cs = slice(c * 64, (c + 1) * 64)
            px = C.ps(0, 4)
            C.mm(px[0:64, 0:64], bt[:, cs], M[:], start=True, stop=False)
            C.mm(px[0:64, 0:64], AakT[:, c, :], vtok[:, c, :], start=False, stop=True)
            C.cp('dve', X0[:], px[0:64, 0:64])
            C.ts('dve', MpL[:], M[:], pp[:, c * 64 + 63:c * 64 + 64], ALU.mult)
            pu = C.ps(0, 4)
            C.mm(pu[0:64, 0:64], Wc[:, c, :], X0[:])
            C.cp('dve', U[:], pu[0:64, 0:64])
            py = C.ps(0, 4)
            C.mm(py[0:64, 0:64], rt[:, cs], M[:], start=True, stop=False)
            C.mm(py[0:64, 0:64], AraT[:, c, :], U[:], start=False, stop=False)
            C.mm(py[0:64, 0:64], ArkT[:, c, :], vtok[:, c, :], start=False, stop=True)
            pm = C.ps(0, 4)
            C.mm(pm[0:64, 0:64], atok[:, c, :], U[:], start=True, stop=False)
            C.mm(pm[0:64, 0:64], ktok[:, c, :], vtok[:, c, :], start=False, stop=True)
            C.cp('act', ytok[:, c, :], py[0:64, 0:64])
            C.stt('dve', M[:], pm[0:64, 0:64], pp[:, c * 64 + 63:c * 64 + 64], MpL[:], ALU.mult, ALU.add)
        C.memset('pool', st8[:], 0.0)
        for c in range(NC):
            C.act(yo[:, c, :], ytok[:, c, :], AF.Identity, accum=st8[:, c:c + 1])
        C.ts('dve', st8[:, NC:2 * NC], st8[:, 0:NC], -1.0 / 64, ALU.mult)
        for c in range(NC):
            C.ts('dve', ytok[:, c, :], ytok[:, c, :], st8[:, NC + c:NC + c + 1], ALU.add)
            C.act(yo[:, c, :], ytok[:, c, :], AF.Square, accum=st8[:, 2 * NC + c:2 * NC + c + 1])
        C.act(st8[:, 3 * NC:4 * NC], st8[:, 2 * NC:3 * NC], AF.Sqrt, bias=epsg[:, 0:1], scale=1.0 / 64)
        C.recip(st8[:, 3 * NC:4 * NC], st8[:, 3 * NC:4 * NC])
        for c in range(NC):
            C.stt('dve', yo[:, c, :], ytok[:, c, :], st8[:, 3 * NC + c:3 * NC + c + 1], rln[:, 0:64], ALU.mult, ALU.mult)
            C.tt('pool', yo[:, c, :], yo[:, c, :], rln[:, 64:128], ALU.add)
            C.stt('dve', yo[:, c, :], vtok[:, c, :], bsum[:, c:c + 1], yo[:, c, :], ALU.mult, ALU.add)
            C.tt('pool', yo[:, c, :], yo[:, c, :], gtok[:, c, :], ALU.mult)
        C.dma(yr_d[t0:t0 + TB, :].rearrange("(c t) v -> t c v", c=NC), yo[:])

LN16 = 2.772588722239781


def build_B(T8=8192):
    nc = bass.Bass("TRN2", target_bir_lowering=False)
    di = lambda n, s: nc.dram_tensor(n, s, F32, kind="ExternalInput").ap()
    do = lambda n, s: nc.dram_tensor(n, s, F32, kind="ExternalOutput").ap()
    zqk_d = di("zqk", [512, T8]); convw_d = di("convw", [128, 16]); vtm_d = di("vtm", [T8, 128])
    zi_d = di("zi", [1, T8]); zf_d = di("zf", [1, T8]); gb_d = di("gb", [64, 2])
    hm_d = do("hm", [T8, 128])
    zs_d = di("zs", [64, T8]); s5p_d = di("s5p", [128, 6]); s5b_d = di("s5b", [64, 512]); s5c_d = di("s5c", [128, 256])
    s5d_d = di("s5d", [64, 1]); ys_d = do("ys", [64, T8])
    zr_d = di("zr", [640, T8]); rp_d = di("rp", [128, 16]); rw2_d = di("rw2", [96, 128]); rg2_d = di("rg2", [128, 128])
    rln_d = di("rln", [64, 128]); yr_d = do("yr", [T8, 64])
    C = Ctx(nc)
    C.wrr = 0
    make_consts(C)
    sb = C.sb
    NT = T8 // 128
    convw = sb("convw", [128, 16]); C.dma(convw[:], convw_d)
    gb = sb("gb", [64, 2]); C.dma(gb[:], gb_d)
    gb15 = sb("gb15", [64, 2]); C.ts('dve', gb15[:], gb[:], 1.0 / 15, ALU.mult)
    qkT = sb("qkT", [128, 4, T8], BF16)
    vaug = sb("vaug", [128, NT, 129], BF16)
    C.memset('pool', vaug[:, :, 128:129], 1.0)
    C.dma(vaug[:, :, 0:128], vtm_d.rearrange("(n p) d -> p n d", p=128), q='pool')
    CH = 1024 if T8 >= 1024 else T8
    cbuf = [sb("cbuf%d" % i, [128, CH + 3]) for i in range(2)]
    cacc = [sb("cacc%d" % i, [128, CH]) for i in range(2)]
    bi = 0
    for tl in range(4):
        for t0 in range(0, T8, CH):
            cb = cbuf[bi % 2]; ca = cacc[bi % 2]; bi += 1
            if t0 == 0:
                C.memset('pool', cb[:, 0:3], 0.0)
                C.dma(cb[:, 3:3 + CH], zqk_d[tl * 128:(tl + 1) * 128, 0:CH])
            else:
                C.dma(cb[:, 0:3 + CH], zqk_d[tl * 128:(tl + 1) * 128, t0 - 3:t0 + CH])
            C.ts('dve', ca[:], cb[:, 3:3 + CH], convw[:, tl * 4:tl * 4 + 1], ALU.mult)
            for j in range(1, 4):
                C.stt('dve', ca[:], cb[:, 3 - j:3 - j + CH], convw[:, tl * 4 + j:tl * 4 + j + 1], ca[:], ALU.mult, ALU.add)
            C.act(qkT[:, tl, t0:t0 + CH], ca[:], AF.Silu)
    nbscr = nc.dram_tensor("nbscr", [T8], F32)
    zi = sb("zi", [NT, 128]); zf = sb("zf", [NT, 128])
    C.dma(zi[:], zi_d.rearrange("o (p j) -> (o p) j", j=128)); C.dma(zf[:], zf_d.rearrange("o (p j) -> (o p) j", j=128))
    C.act(zi[:], zi[:], AF.Tanh, bias=gb15[0:NT, 0:1], scale=1.0 / 15)
    C.act(zf[:], zf[:], AF.Tanh, bias=gb15[0:NT, 1:2], scale=1.0 / 15)
    C.act(zf[:], zf[:], AF.Exp, scale=-15.0)
    C.act(zf[:], zf[:], AF.Ln, bias=C.cb[0:NT, 0:1])
    nb = sb("nb", [NT, 128])
    C.scan(nb[:], C.ones[0:NT, 0:128], zf[:], 0.0)
    ustr = sb("ustr", [64, 64]); C.memset('pool', ustr[:], 1.0)
    C.asel(ustr[:], ustr[:], [[1, 64]], ALU.is_gt, 0.0, base=0, cm=-1)
    tot = sb("gtot", [NT, 1]); C.cp('dve', tot[:], nb[:, 127:128])
    pso = C.psb[0]
    C.mm(pso[0:NT, 0:1], ustr[0:NT, 0:NT], tot[:])
    offs = sb("goffs", [NT, 1]); C.cp('dve', offs[:], pso[0:NT, 0:1])
    C.ts('dve', nb[:], nb[:], offs[:, 0:1], ALU.add)
    C.dma(nbscr.ap().rearrange("(p j) -> p j", j=128), nb[:])
    crow = sb("crow", [NT, 128])
    C.stt('dve', crow[:], zi[:], 15.0, nb[:], ALU.mult, ALU.add)
    C.ts('dve', crow[:], crow[:], -LN16, ALU.add)
    cT = sb("cT", [128, NT])
    pst = C.psb[1]
    C.tr(pst[:, 0:NT], crow[:], C.ident[0:NT, 0:NT])
    C.cp('dve', cT[:], pst[:, 0:NT])
    nbrow = nbscr.ap().rearrange("(o t) -> o t", o=1)
    bBc = [sb("bBc%d" % i, [128, 512]) for i in range(2)]
    negtri = sb("negtri", [128, 128]); C.memset('pool', negtri[:], 0.0)
    C.asel(negtri[:], negtri[:], [[1, 128]], ALU.is_ge, -30000.0, base=0, cm=-1)
    dtb = [sb("dtb%d" % i, [128, 512]) for i in range(2)]
    ptb = [sb("ptb%d" % i, [128, 512], BF16) for i in range(2)]
    dgt = [sb("dgt%d" % i, [128, 128]) for i in range(2)]
    hmo = [sb("hmo%d" % i, [128, 128]) for i in range(2)]
    den = sb("mden", [128, 8])
    it = 0
    NQ = min(512, T8)
    QT = NQ // 128
    for ch in range(T8 // NQ):
        q0 = ch * NQ
        accs = [C.psb[4 + j] for j in range(QT)]
        bB = bBc[ch % 2]
        C.dma(bB[:, 0:NQ], nbrow[:, q0:q0 + NQ].to_broadcast([128, NQ]))
        for ks in range(QT * ch + QT):
            m = max(0, ks - QT * ch)
            lo = m * 128
            ps = C.ps(0, 4)
            for kd in range(2):
                C.mm(ps[:, lo:NQ], qkT[:, 2 + kd, ks * 128:(ks + 1) * 128], qkT[:, kd, q0 + lo:q0 + NQ], start=(kd == 0), stop=(kd == 1))
            dt = dtb[it % 2]; pt = ptb[it % 2]; dg = dgt[it % 2]; it += 1
            if ks >= QT * ch:
                C.tt('pool', dg[:], negtri[:], bB[:, lo:lo + 128], ALU.subtract)
                C.act(dt[:, lo:lo + 128], dg[:], AF.Exp, bias=cT[:, ks:ks + 1])
                if lo + 128 < NQ:
                    C.act(dt[:, lo + 128:NQ], bB[:, lo + 128:NQ], AF.Exp, bias=cT[:, ks:ks + 1], scale=-1.0)
            else:
                C.act(dt[:, lo:NQ], bB[:, lo:NQ], AF.Exp, bias=cT[:, ks:ks + 1], scale=-1.0)
            C.tt('dve', pt[:, lo:NQ], ps[:, lo:NQ], dt[:, lo:NQ], ALU.mult)
            for j in range(m, QT):
                C.mm(accs[j][:, 0:129], pt[:, j * 128:(j + 1) * 128], vaug[:, ks, :], start=(ks == 0), stop=(ks == QT * ch + j))
        for j in range(QT):
            d = den[:, j:j + 1]
            C.act(d, accs[j][:, 128:129], AF.Abs)
            C.ts('dve', d, d, 1.0, ALU.max)
            C.recip(d, d)
            ho = hmo[j % 2]
            C.ts('dve', ho[:], accs[j][:, 0:128], d, ALU.mult)
            C.dma(hm_d[q0 + j * 128:q0 + (j + 1) * 128, :], ho[:])
    build_S5(C, T8, zs_d, s5p_d, s5b_d, s5c_d, s5d_d, ys_d)
    build_RWKV(C, T8, zr_d, rp_d, rw2_d, rg2_d, rln_d, yr_d)
    nw = C.finish([hm_d, ys_d, yr_d])
    return nc, nw

M_IN = 4104
RWKV_IN = 1984
R0 = M_IN
S0 = M_IN + RWKV_IN
G0 = S0 + 512
NIN = 12744


def b_inputs(z, inp, l, c):
    f = np.float32
    hd, half = c // 2, c % 2
    T = z.shape[0]
    A = np.ascontiguousarray
    d = {}
    d["zqk"] = A(np.concatenate([z[:, hd * 256:(hd + 1) * 256].T, z[:, 1024 + hd * 256:1024 + (hd + 1) * 256].T], 0))
    cw = np.zeros((128, 16), f)
    conv = inp["mlstm_conv"][l]
    for tl in range(4):
        base = (hd * 256 + tl * 128) if tl < 2 else (1024 + hd * 256 + (tl - 2) * 128)
        for j in range(4):
            cw[:, tl * 4 + j] = conv[j, base:base + 128]
    d["convw"] = cw
    vc = 2048 + hd * 256 + half * 128
    d["vtm"] = A(z[:, vc:vc + 128])
    d["zi"] = A(z[:, 4096 + hd][None, :])
    d["zf"] = A(z[:, 4100 + hd][None, :])
    d["gb"] = np.tile(np.array([[inp["mlstm_ib"][l, hd], inp["mlstm_fb"][l, hd]]], f), (64, 1))
    d["zs"] = A(z[:, S0 + c * 64:S0 + (c + 1) * 64].T)
    s5p = np.zeros((128, 6), f)
    s5b = np.zeros((64, 512), f)
    s5c = np.zeros((128, 256), f)
    for j in range(2):
        for g2 in range(2):
            gl = 2 * j + g2
            g = 4 * c + gl
            rows = slice(g2 * 64, (g2 + 1) * 64)
            s5p[rows, 0 + j] = inp["s5_a_re"][l, g]
            s5p[rows, 2 + j] = inp["s5_a_im"][l, g]
            s5p[rows, 4 + j] = inp["s5_log_dt"][l, g]
            s5b[gl * 16:(gl + 1) * 16, j * 128 + g2 * 64:j * 128 + (g2 + 1) * 64] = inp["s5_b_re"][l, g].T
            s5b[gl * 16:(gl + 1) * 16, 256 + j * 128 + g2 * 64:256 + j * 128 + (g2 + 1) * 64] = inp["s5_b_im"][l, g].T
            s5c[rows, j * 64 + gl * 16:j * 64 + (gl + 1) * 16] = inp["s5_c_re"][l, g].T
            s5c[rows, 128 + j * 64 + gl * 16:128 + j * 64 + (gl + 1) * 16] = inp["s5_c_im"][l, g].T
    d["s5p"], d["s5b"], d["s5c"] = s5p, s5b, s5c
    d["s5d"] = A(inp["s5_d"][l, c * 64:(c + 1) * 64][:, None])
    hc = slice(c * 64, (c + 1) * 64)
    cols = [(R0 + c * 64, 64), (R0 + 512 + c * 64, 64), (R0 + 1024 + c * 64, 64), (R0 + 1536, 96), (R0 + 1632, 96),
            (R0 + 1728, 128), (R0 + 1856, 128)]
    d["zr"] = A(np.concatenate([z[:, o:o + n].T for o, n in cols], 0))
    rp = np.zeros((128, 16), f)
    mu = inp["rwkv_mu"][l]
    for i, (o, n) in enumerate(cols):
        rp[:n, i] = mu[o - R0:o - R0 + n]
    for k, name in ((7, "rwkv_w0"), (8, "rwkv_a0"), (9, "rwkv_kk"), (10, "rwkv_ka"), (11, "rwkv_rk")):
        rp[:64, k] = inp[name][l, hc]
    d["rp"] = rp
    d["rw2"] = A(np.concatenate([inp["rwkv_w2"][l][:, hc], inp["rwkv_a2"][l][:, hc]], 1))
    g2w = inp["rwkv_g2"][l][:, hc]
    d["rg2"] = A(np.concatenate([g2w[0:128], g2w[128:256]], 1))
    d["rln"] = A(np.concatenate([np.tile(inp["rwkv_ln_g"][l, hc][None, :], (64, 1)), np.tile(inp["rwkv_ln_b"][l, hc][None, :], (64, 1))], 1))
    return {k: A(v.astype(f)) for k, v in d.items()}

def dense_multi(C, streams, col0, ncols, T, epi, gcols=128):
    c = col0
    while c < col0 + ncols:
        gw = min(gcols, col0 + ncols - c)
        views = []
        for (w2d, K, rhs, wbufs) in streams:
            KT = K // 128
            wb = wbufs[C.wrr % len(wbufs)]
            wbv = wb[:, 0:KT * gw].rearrange("p (kt n) -> p kt n", kt=KT)
            C.dma(wbv, w2d.rearrange("(kt p) n -> p kt n", p=128)[:, :, c:c + gw], q='pool')
            views.append(wbv)
        C.wrr += 1
        for cc in range(0, gw, 128):
            cw = min(128, gw - cc)
            banks = []
            for si, (w2d, K, rhs, wbufs) in enumerate(streams):
                KT = K // 128
                ps = C.psb[(C.psi % 2) * 3 + si]
                for kt in range(KT):
                    C.mm(ps[:cw, :T], views[si][:, kt, cc:cc + cw], rhs(kt)[:, 0:T], start=(kt == 0), stop=(kt == KT - 1))
                banks.append(ps)
            C.psi += 1
            epi(c + cc, cw, banks)
        c += gw


def build_C(final=False, TH=512, NH=2):
    nc = bass.Bass("TRN2", target_bir_lowering=False)
    D = 2048; KT = 16; FF = 5632
    T = TH * NH
    di = lambda n, s: nc.dram_tensor(n, s, F32, kind="ExternalInput").ap()
    hT_d = di("hT", [D, T]); hm_d = di("hmT", [1024, T]); zo_d = di("zoT", [1024, T]); yr_d = di("yrT", [512, T])
    ys_d = di("ysT", [512, T]); zg_d = di("zgT", [6144, T]); p_d = di("pT", [256, T])
    vec_d = di("vecs", [128, 80])
    glu_d = di("glu_w", [512, 512]); upm_d = di("w_up_m", [1024, D]); upr_d = di("w_up_r", [512, D]); ups_d = di("w_up_s", [512, D])
    wout_d = di("w_out", [D, D]); wg_d = di("ffn_w_gate", [D, FF]); wu_d = di("ffn_w_up", [D, FF]); wd_d = di("ffn_w_down", [FF, D])
    pg_d = di("ple_w_gate", [D, D]); pp_d = di("ple_w_proj", [256, D])
    out_d = nc.dram_tensor("hout", [D, T], F32, kind="ExternalOutput").ap()
    C = Ctx(nc)
    C.wrr = 0
    make_consts(C)
    C.epsb = C.cb[:, 2:3]
    sb = C.sb
    vec = sb("vecs", [128, 80]); C.dma(vec[:], vec_d)
    hT = sb("hT", [128, KT, TH])
    xn = sb("xn", [128, KT, TH], BF16)
    sq = sb("sq", [128, 2, TH]); rstd = sb("rstd", [128, TH])
    ym = sb("ym", [128, 8, TH], BF16); yr = sb("yr", [128, 4, TH], BF16); ysb = sb("ysb", [128, 4, TH], BF16)
    ys32 = sb("ys32", [128, 4, TH])
    mixed = sb("mixed", [128, KT, TH], BF16)
    hid = sb("hid", [128, 44, TH], BF16)
    pb = sb("pb", [128, 2, TH], BF16)
    ld = [sb("ld%d" % i, [128, TH]) for i in range(6)]
    wA = [sb("wA%d" % i, [128, 44 * 128], BF16) for i in range(2)]
    wB = [sb("wB%d" % i, [128, 16 * 128], BF16) for i in range(2)]
    wC = [sb("wC%d" % i, [128, 16 * 128], BF16) for i in range(2)]
    li = [0]

    def nld():
        li[0] += 1
        return ld[li[0] % 6]
    for hf in range(NH):
        ts_ = slice(hf * TH, (hf + 1) * TH)
        hv = hT_d.rearrange("(kt p) t -> p kt t", p=128)
        for kt in range(KT):
            C.dma(hT[:, kt, :], hv[:, kt, ts_])
        C.dma(yr[:], yr_d.rearrange("(kt p) t -> p kt t", p=128)[:, :, ts_], q='pool')
        C.dma(ysb[:], ys_d.rearrange("(kt p) t -> p kt t", p=128)[:, :, ts_], q='pool')
        C.dma(ys32[:], ys_d.rearrange("(kt p) t -> p kt t", p=128)[:, :, ts_])
        C.dma(pb[:], p_d.rearrange("(kt p) t -> p kt t", p=128)[:, :, ts_], q='pool')
        for hd in range(4):
            tl = []
            bank = C.psb[6]
            for k2 in range(2):
                t_ = nld(); C.dma(t_[:], hm_d[(hd * 2 + k2) * 128:(hd * 2 + k2 + 1) * 128, ts_]); tl.append(t_)
                s = sq[:, k2, :]
                C.act(s, t_[:], AF.Square)
                C.mm(bank[:, :TH], C.ones[:], s, start=(k2 == 0), stop=(k2 == 1))
            C.act(rstd[:], bank[:, :TH], AF.Sqrt, bias=C.epsb, scale=1.0 / 256)
            C.recip(rstd[:], rstd[:])
            for k2 in range(2):
                ct = hd * 2 + k2
                o_ = nld(); C.dma(o_[:], zo_d[ct * 128:(ct + 1) * 128, ts_])
                C.act(o_[:], o_[:], AF.Sigmoid)
                C.stt('dve', tl[k2][:], tl[k2][:], vec[:, ct:ct + 1], rstd[:], ALU.mult, ALU.mult)
                C.tt('dve', ym[:, ct, :], tl[k2][:], o_[:], ALU.mult)
        def epi_glu(c0, cw, banks):
            ct = c0 // 128
            t_ = nld()
            C.act(t_[:], banks[0][:, :TH], AF.Sigmoid, bias=vec[:, 8 + ct:9 + ct])
            C.tt('dve', ysb[:, ct, :], t_[:], ys32[:, ct, :], ALU.mult)
        ysg = hid[:, 0:4, :]
        def epi_glu2(c0, cw, banks):
            ct = c0 // 128
            t_ = nld()
            C.act(t_[:], banks[0][:, :TH], AF.Sigmoid, bias=vec[:, 8 + ct:9 + ct])
            C.tt('dve', ysg[:, ct, :], t_[:], ys32[:, ct, :], ALU.mult)
        dense_multi(C, [(glu_d, 512, lambda kt: ysb[:, kt, :], wB)], 0, 512, TH, epi_glu2)
        C.cp('pool', ysb[:], ysg)
        def epi_mix(c0, cw, banks):
            ct = c0 // 128
            acc = nld()
            for bi in range(3):
                g_ = nld(); C.dma(g_[:], zg_d[bi * 2048 + c0:bi * 2048 + c0 + 128, ts_])
                C.act(g_[:], g_[:], AF.Sigmoid)
                if bi == 0:
                    C.tt('dve', acc[:], g_[:], banks[bi][:, :TH], ALU.mult)
                else:
                    C.tt('dve', g_[:], g_[:], banks[bi][:, :TH], ALU.mult)
                    C.tt('pool', acc[:], acc[:], g_[:], ALU.add)
            C.cp('act', mixed[:, ct, :], acc[:])
        dense_multi(C, [(upm_d, 1024, lambda kt: ym[:, kt, :], wA), (upr_d, 512, lambda kt: yr[:, kt, :], wB), (ups_d, 512, lambda kt: ysb[:, kt, :], wC)], 0, D, TH, epi_mix)
        def epi_res(c0, cw, banks):
            ct = c0 // 128
            C.tt('dve', hT[:, ct, :], hT[:, ct, :], banks[0][:, :TH], ALU.add)
        dense_multi(C, [(wout_d, D, lambda kt: mixed[:, kt, :], wB)], 0, D, TH, epi_res)
        rmsnorm_fm(C, hT, KT, TH, vec[:, 16:32], xn, sq, rstd)
        def epi_ffn(c0, cw, banks):
            ct = c0 // 128
            t_ = nld()
            C.act(t_[:], banks[0][:, :TH], AF.Silu)
            C.tt('dve', hid[:, ct, :], t_[:], banks[1][:, :TH], ALU.mult)
        dense_multi(C, [(wg_d, D, lambda kt: xn[:, kt, :], wB), (wu_d, D, lambda kt: xn[:, kt, :], wC)], 0, FF, TH, epi_ffn)
        dense_multi(C, [(wd_d, FF, lambda kt: hid[:, kt, :], wA)], 0, D, TH, epi_res)
        rmsnorm_fm(C, hT, KT, TH, vec[:, 32:48], xn, sq, rstd)
        def epi_ple(c0, cw, banks):
            ct = c0 // 128
            t_ = nld()
            C.act(t_[:], banks[0][:, :TH], AF.Sigmoid)
            C.tt('dve', t_[:], t_[:], banks[1][:, :TH], ALU.mult)
            C.tt('pool', hT[:, ct, :], hT[:, ct, :], t_[:], ALU.add)
        dense_multi(C, [(pg_d, D, lambda kt: xn[:, kt, :], wB), (pp_d, 256, lambda kt: pb[:, kt, :], wC)], 0, D, TH, epi_ple)
        ov = out_d.rearrange("(kt p) t -> p kt t", p=128)
        if final:
            rmsnorm_fm(C, hT, KT, TH, vec[:, 48:64], xn, sq, rstd)
            for kt in range(KT):
                t_ = nld()
                C.stt('dve', t_[:], hT[:, kt, :], vec[:, 48 + kt:49 + kt], rstd[:], ALU.mult, ALU.mult)
                C.dma(ov[:, kt, ts_], t_[:])
        else:
            for kt in range(KT):
                C.dma(ov[:, kt, ts_], hT[:, kt, :])
    nw = C.finish([out_d])
    return nc, nw

_CACHE = {}


def _prog(name, fn):
    if name not in _CACHE:
        _CACHE[name] = fn()[0]
    return _CACHE[name]


def _run(nc, maps):
    res = run_bass_kernel_spmd(nc, maps, core_ids=list(range(8)))
    return res.results


def kernel(**inp):
    inp = {k: np.asarray(v) for k, v in inp.items()}
    f = np.float32
    A = np.ascontiguousarray
    h = inp["x"][0].astype(f)
    NCORE = 8
    TS = 1024
    vt = lambda v: A(v.reshape(-1, 128).T)
    for l in range(2):
        ncA = _prog("A", build_A)
        gA = vt(inp["norm_mix_g"][l])
        maps = [{"hT": A(h[c * TS:(c + 1) * TS].T), "g": gA, "w": inp["w_in"][l]} for c in range(NCORE)]
        rA = _run(ncA, maps)
        z = np.concatenate([r["zT"].T for r in rA], 0)
        ncB = _prog("B", build_B)
        rB = _run(ncB, [b_inputs(z, inp, l, c) for c in range(NCORE)])
        hm = np.zeros((8192, 1024), f); yr = np.zeros((8192, 512), f); ys = np.zeros((8192, 512), f)
        for c in range(NCORE):
            hd, half = c // 2, c % 2
            hm[:, hd * 256 + half * 128:hd * 256 + half * 128 + 128] = rB[c]["hm"]
            yr[:, c * 64:(c + 1) * 64] = rB[c]["yr"]
            ys[:, c * 64:(c + 1) * 64] = rB[c]["ys"].T
        final = (l == 1)
        ncC = _prog("C%d" % final, lambda: build_C(final=final))
        vecs = np.zeros((128, 80), f)
        vecs[:, 0:8] = vt(inp["mlstm_norm_g"][l]); vecs[:, 8:12] = vt(inp["s5_glu_b"][l])
        vecs[:, 16:32] = vt(inp["norm_ffn_g"][l]); vecs[:, 32:48] = vt(inp["norm_ple_g"][l]); vecs[:, 48:64] = vt(inp["final_norm_g"])
        maps = []
        for c in range(NCORE):
            sl = slice(c * TS, (c + 1) * TS)
            m = {"hT": A(h[sl].T), "hmT": A(hm[sl].T), "zoT": A(z[sl, 3072:4096].T), "yrT": A(yr[sl].T), "ysT": A(ys[sl].T),
                 "zgT": A(z[sl, G0:].T), "pT": A(inp["p"][l, 0, sl].T), "vecs": vecs}
            for k in ("w_up_m", "w_up_r", "w_up_s", "w_out", "ffn_w_gate", "ffn_w_up", "ffn_w_down", "ple_w_gate", "ple_w_proj"):
                m[k] = inp[k][l]
            m["glu_w"] = inp["s5_glu_w"][l]
            maps.append({k: A(v.astype(f)) for k, v in m.items()})
        rC = _run(ncC, maps)
        h = np.concatenate([r["hout"].T for r in rC], 0)
    return h[None].astype(np.float32)
```

```python
import numpy as np
import concourse.bass as bass
import concourse.mybir as mybir
from concourse.bass_utils import run_bass_kernel_spmd

F32 = mybir.dt.float32
BF16 = mybir.dt.bfloat16
AF = mybir.ActivationFunctionType
ALU = mybir.AluOpType
AX = mybir.AxisListType


def _box(ap):
    t = ap.tensor
    shp = tuple(t.shape)
    space = str(ap.space)
    if 'DRAM' in space.upper() or 'HBM' in space.upper():
        F = 1 << 62
    else:
        F = 1
        for s in shp[1:]:
            F *= int(s)
    off = int(ap.offset)
    p0 = off // F
    f0 = off % F
    ps = 0
    fs = 0
    for step, cnt in ap.ap:
        step = int(step)
        cnt = int(cnt)
        if cnt <= 1 or step == 0:
            continue
        if step % F == 0:
            ps += (cnt - 1) * (step // F)
        else:
            fs += (cnt - 1) * step
    if 'PSUM' in space.upper() or t.name.startswith('psb'):
        return (t.name, 0, 127, 0, F - 1)
    return (t.name, p0, p0 + ps, f0, f0 + fs)


def _ovl(a, b):
    return not (a[2] < b[1] or b[2] < a[1] or a[4] < b[3] or b[4] < a[3])


def _cov(a, b):
    return a[1] <= b[1] and a[2] >= b[2] and a[3] <= b[3] and a[4] >= b[4]


class Prog:
    def __init__(self, nc, n_dma_sems=24):
        self.nc = nc
        self.ops = []
        self.eng = dict(pe=nc.tensor, dve=nc.vector, act=nc.scalar, pool=nc.gpsimd, sp=nc.sync)
        self.n_dma_sems = n_dma_sems
        self.acc = {}

    def op(self, eng, fn, reads=(), writes=(), dma=False, pe_acc=False, inc=16):
        i = len(self.ops)
        deps = set()
        rb = [_box(a) for a in reads]
        wb = [_box(a) for a in writes]
        for b in rb:
            for (ob, oi, ow) in self.acc.get(b[0], ()):
                if ow and _ovl(b, ob):
                    deps.add(oi)
        for b in wb:
            for (ob, oi, ow) in self.acc.get(b[0], ()):
                if _ovl(b, ob):
                    deps.add(oi)
        deps.discard(i)
        for b in wb:
            lst = self.acc.setdefault(b[0], [])
            lst[:] = [e for e in lst if not _cov(b, e[0])]
            lst.append((b, i, True))
        for b in rb:
            lst = self.acc.setdefault(b[0], [])
            lst[:] = [e for e in lst if not (not e[2] and e[0] == b and e[1] < i and self.ops[e[1]]['eng'] == eng and not self.ops[e[1]]['dma'] and not dma)]
            lst.append((b, i, False))
        if pe_acc:
            deps = {d for d in deps if not (self.ops[d]['eng'] == 'pe' and self.ops[d]['pe'])}
        self.ops.append(dict(eng=eng, fn=fn, deps=deps, dma=dma, pe=(eng == 'pe'), inc=inc))
        return i

    def emit(self):
        nc = self.nc
        ops = self.ops
        needed = [False] * len(ops)
        for o in ops:
            for d in o['deps']:
                needed[d] = True
        esem = {e: nc.alloc_semaphore('sem_' + e) for e in self.eng}
        dsem = [nc.alloc_semaphore('dsem%d' % k) for k in range(self.n_dma_sems)]
        semid = {}
        sems = []
        for e in self.eng:
            semid[('e', e)] = len(sems)
            sems.append(esem[e])
        for k in range(self.n_dma_sems):
            semid[('d', k)] = len(sems)
            sems.append(dsem[k])
        ns = len(sems)
        ecount = {e: 0 for e in self.eng}
        dcount = [0] * self.n_dma_sems
        dnext = 0
        seen = {e: [0] * ns for e in self.eng}
        sig = [None] * len(ops)
        know = [None] * len(ops)
        nwait = 0
        for i, o in enumerate(ops):
            e = o['eng']
            E = self.eng[e]
            sv = seen[e]
            req = {}
            for d in o['deps']:
                si, val = sig[d]
                if sv[si] >= val:
                    continue
                if req.get(si, 0) < val:
                    req[si] = val
            dk = None
            if o['dma']:
                dk = dnext
                dnext = (dnext + 1) % self.n_dma_sems
                si = semid[('d', dk)]
                if dcount[dk] > sv[si]:
                    req[si] = max(req.get(si, 0), dcount[dk])
            for d in o['deps']:
                si, val = sig[d]
                if si in req and req[si] <= val:
                    kd = know[d]
                    if kd is not None:
                        for sj in list(req.keys()):
                            if sj != si and kd[sj] >= req[sj]:
                                del req[sj]
            for si, val in req.items():
                E.wait_ge(sems[si], val)
                nwait += 1
                if sv[si] < val:
                    sv[si] = val
            for d in o['deps']:
                kd = know[d]
                if kd is not None:
                    for sj in range(ns):
                        if kd[sj] > sv[sj]:
                            sv[sj] = kd[sj]
            ins = o['fn'](E)
            if o['dma']:
                dcount[dk] += o['inc']
                ins.then_inc(dsem[dk], o['inc'])
                si = semid[('d', dk)]
                sig[i] = (si, dcount[dk])
                know[i] = list(sv)
            else:
                si = semid[('e', e)]
                if needed[i]:
                    ecount[e] += 1
                    ins.then_inc(esem[e], 1)
                    sig[i] = (si, ecount[e])
                    k = list(sv)
                    k[si] = ecount[e]
                    know[i] = k
                else:
                    sig[i] = (si, ecount[e] + 1)
                    know[i] = None
        self.final = (sems, semid, dcount, ecount)
        self.nwait = nwait
        return nwait

    def finish(self, eng='sp'):
        pass
EPS = 1e-6


class Ctx:
    def __init__(self, nc, n_dma_sems=24):
        self.nc = nc
        self.P = Prog(nc, n_dma_sems)
        self.dummy = nc.alloc_semaphore("dummy_fin")
        self.rr = 0
        self.psb = [nc.alloc_psum_tensor("psb%d" % i, [128, 512], F32) for i in range(8)]
        self.psi = 0

    def sb(self, name, shape, dt=F32):
        return self.nc.alloc_sbuf_tensor("s_" + name, list(shape), dt)

    def ps(self, lo=0, hi=8):
        b = self.psb[lo + (self.psi % (hi - lo))]
        self.psi += 1
        return b

    def dma(self, out, in_, q='sp'):
        self.P.op(q, lambda E: E.dma_start(out=out, in_=in_), reads=[in_], writes=[out], dma=True)

    def mm(self, out, lhsT, rhs, start=True, stop=True):
        self.P.op('pe', lambda E: E.matmul(out, lhsT, rhs, start=start, stop=stop),
                  reads=[lhsT, rhs], writes=[out], pe_acc=not start)

    def tr(self, out, in_, ident):
        self.P.op('pe', lambda E: E.transpose(out, in_, ident), reads=[in_, ident], writes=[out])

    def act(self, out, in_, func, bias=None, scale=1.0, accum=None, eng='act'):
        rd = [in_]
        kw = {}
        if bias is not None:
            kw['bias'] = bias
            if not isinstance(bias, (int, float)):
                rd.append(bias)
        if not isinstance(scale, (int, float)):
            rd.append(scale)
        wr = [out]
        if accum is not None:
            kw['accum_out'] = accum
            wr.append(accum)
        self.P.op('act', lambda E: E.activation(out, in_, func, scale=scale, **kw), reads=rd, writes=wr)

    def tt(self, eng, out, a, b, op):
        self.P.op(eng, lambda E: E.tensor_tensor(out, a, b, op), reads=[a, b], writes=[out])

    def ts(self, eng, out, a, s1, op0, s2=None, op1=None, accum=None):
        rd = [a] + [s for s in (s1, s2) if s is not None and not isinstance(s, (int, float))]
        wr = [out] + ([accum] if accum is not None else [])
        kw = {}
        if op1 is not None:
            kw['op1'] = op1
        if accum is not None:
            kw['accum_out'] = accum
        self.P.op(eng, lambda E: E.tensor_scalar(out, a, s1, s2, op0, **kw), reads=rd, writes=wr)

    def stt(self, eng, out, in0, scalar, in1, op0, op1):
        rd = [in0, in1] + ([scalar] if not isinstance(scalar, (int, float)) else [])
        eng = 'dve'
        self.P.op(eng, lambda E: E.scalar_tensor_tensor(out, in0, scalar, in1, op0, op1), reads=rd, writes=[out])

    def cp(self, eng, out, in_):
        if eng == 'act' and getattr(self, 'act_copy_ok', True):
            self.P.op('act', lambda E: E.copy(out, in_), reads=[in_], writes=[out])
        else:
            if eng == 'act':
                eng = 'dve'
            self.P.op(eng, lambda E: E.tensor_copy(out, in_), reads=[in_], writes=[out])

    def memset(self, eng, out, val):
        self.P.op(eng, lambda E: E.memset(out, val), writes=[out])

    def recip(self, out, in_):
        self.P.op('dve', lambda E: E.reciprocal(out, in_), reads=[in_], writes=[out])

    def scan(self, out, d0, d1, init, op0=None, op1=None, eng='dve'):
        op0 = op0 or ALU.mult
        op1 = op1 or ALU.add
        rd = [d0, d1] + ([init] if not isinstance(init, (int, float)) else [])
        self.P.op(eng, lambda E: E.tensor_tensor_scan(out, d0, d1, init, op0, op1), reads=rd, writes=[out])

    def asel(self, out, in_, pattern, cmp, fill, base=0, cm=0):
        self.P.op('pool', lambda E: E.affine_select(out, in_, pattern, cmp, fill, base=base, channel_multiplier=cm),
                  reads=[in_], writes=[out])

    def finish(self, outs):
        d = self.dummy
        self.P.op('sp', lambda E: E.sem_inc(d, 1), reads=list(outs))
        return self.P.emit()

    def evac(self, out, in_):
        self.rr += 1
        if self.rr % 2:
            self.cp('dve', out, in_)
        else:
            self.cp('act', out, in_)


def make_consts(C):
    C.ones = C.sb("c_ones", [128, 128])
    C.memset('pool', C.ones[:], 1.0)
    C.ident = C.sb("c_ident", [128, 128])
    C.memset('pool', C.ident[:], 1.0)
    C.asel(C.ident[:], C.ident[:], [[1, 128]], ALU.is_equal, 0.0, base=0, cm=-1)
    C.cb = C.sb("c_cb", [128, 4])
    C.memset('pool', C.cb[:, 0:1], 1.0)
    C.memset('pool', C.cb[:, 1:2], 1.5707963267948966)
    C.memset('pool', C.cb[:, 2:3], EPS)
    C.identb = C.sb("c_identb", [128, 128], BF16)
    C.cp('pool', C.identb[:], C.ident[:])


def rmsnorm_fm(C, hT, KT, T, g_sb, xn, sq_scr, rstd):
    nch = (T + 511) // 512
    banks = [C.psb[6], C.psb[7]]
    for kt in range(KT):
        s = sq_scr[:, kt % 2, :]
        C.act(s, hT[:, kt, :], AF.Square)
        for ch in range(nch):
            n = min(512, T - ch * 512)
            C.mm(banks[ch][:, :n], C.ones[:], s[:, ch * 512:ch * 512 + n], start=(kt == 0), stop=(kt == KT - 1))
    for ch in range(nch):
        n = min(512, T - ch * 512)
        C.act(rstd[:, ch * 512:ch * 512 + n], banks[ch][:, :n], AF.Sqrt, bias=C.epsb[:, 0:1], scale=1.0 / (KT * 128))
    C.recip(rstd[:, :T], rstd[:, :T])
    for kt in range(KT):
        C.stt('dve', xn[:, kt, :], hT[:, kt, :], g_sb[:, kt:kt + 1], rstd[:, :T], ALU.mult, ALU.mult)


def dense_fm(C, w2d, K, col0, ncols, rhs, T, epi, wbufs, gcols=256, tag=""):
    KT = (K + 127) // 128
    wv = w2d.rearrange("(kt p) n -> p kt n", p=128) if K % 128 == 0 else None
    c = col0
    gi = 0
    while c < col0 + ncols:
        gw = min(gcols, col0 + ncols - c)
        wb = wbufs[C.wrr % len(wbufs)]
        C.wrr += 1
        wbv = wb[:, 0:KT * gw].rearrange("p (kt n) -> p kt n", kt=KT)
        if wv is not None:
            C.dma(wbv, wv[:, :, c:c + gw], q='pool')
        else:
            for kt in range(KT):
                kk = min(128, K - kt * 128)
                C.dma(wbv[:kk, kt, :], w2d[kt * 128:kt * 128 + kk, c:c + gw], q='pool')
        for cc in range(0, gw, 128):
            cw = min(128, gw - cc)
            for t0 in range(0, T, 512):
                n = min(512, T - t0)
                ps = C.ps(0, 4)
                for kt in range(KT):
                    kk = min(128, K - kt * 128)
                    C.mm(ps[:cw, :n], wbv[:kk, kt, cc:cc + cw], rhs(kt)[:kk, t0:t0 + n], start=(kt == 0), stop=(kt == KT - 1))
                epi(c + cc, cw, ps, t0, n)
        c += gw
        gi += 1


def build_A(T=1024, D=2048, NIN=12744):
    nc = bass.Bass("TRN2", target_bir_lowering=False)
    KT = D // 128
    hT_d = nc.dram_tensor("hT", [D, T], F32, kind="ExternalInput").ap()
    g_d = nc.dram_tensor("g", [128, KT], F32, kind="ExternalInput").ap()
    w_d = nc.dram_tensor("w", [D, NIN], F32, kind="ExternalInput").ap()
    z_d = nc.dram_tensor("zT", [NIN, T], F32, kind="ExternalOutput").ap()
    C = Ctx(nc)
    C.wrr = 0
    make_consts(C)
    C.epsb = C.sb("epsb", [128, 1])
    C.memset('pool', C.epsb[:], EPS)
    hT = C.sb("hT_sb", [128, KT, T])
    g_sb = C.sb("g_sb", [128, KT])
    xn = C.sb("xn", [128, KT, T], BF16)
    sq = C.sb("sq", [128, 2, T])
    rstd = C.sb("rstd", [128, T])
    wbufs = [C.sb("wb%d" % i, [128, KT * 256], BF16) for i in range(3)]
    zs = [C.sb("zs%d" % i, [128, 512]) for i in range(4)]
    C.dma(g_sb[:], g_d)
    hv = hT_d.rearrange("(kt p) t -> p kt t", p=128)
    for kt in range(KT):
        C.dma(hT[:, kt, :], hv[:, kt, :])
    rmsnorm_fm(C, hT, KT, T, g_sb, xn, sq, rstd)
    st = [0]

    def epi(c0, cw, ps, t0, n):
        z = zs[st[0] % 4]
        st[0] += 1
        C.evac(z[:cw, :n], ps[:cw, :n])
        C.dma(z_d[c0:c0 + cw, t0:t0 + n], z[:cw, :n])
    dense_fm(C, w_d, D, 0, NIN, lambda kt: xn[:, kt, :], T, epi, wbufs)
    nw = C.finish([z_d])
    return nc, nw

TWO_PI = 6.283185307179586


def build_S5(C, T8, zs_d, s5p_d, s5b_d, s5c_d, s5d_d, ys_d):
    sb = C.sb
    TB = min(512, T8)
    NL = TB.bit_length() - 1
    pr = sb("s5p", [128, 6]); C.dma(pr[:], s5p_d)
    bw32 = sb("s5b32", [64, 512]); C.dma(bw32[:], s5b_d)
    cw32 = sb("s5c32", [128, 256]); C.dma(cw32[:], s5c_d)
    dsk = sb("s5d", [64, 1]); C.dma(dsk[:], s5d_d)
    bw = sb("s5bw", [64, 512], BF16); C.cp('pool', bw[:], bw32[:])
    cw = sb("s5cw", [128, 256], BF16)
    C.cp('pool', cw[:, 0:128], cw32[:, 0:128])
    C.ts('dve', cw[:, 128:256], cw32[:, 128:256], -1.0, ALU.mult)
    v = sb("s5v", [128, 40])
    A_RE, A_IM, LDT = pr[:, 0:2], pr[:, 2:4], pr[:, 4:6]
    dt = v[:, 0:2]; C.act(dt, LDT, AF.Exp)
    mag = v[:, 2:4]; C.tt('dve', mag, A_RE, dt, ALU.mult); C.act(mag, mag, AF.Exp)
    ang = v[:, 4:6]; C.tt('dve', ang, A_IM, dt, ALU.mult)
    sn = v[:, 6:8]; cs = v[:, 8:10]
    th = v[:, 28:30]; C.ts('dve', th, ang, 1.0 / 16, ALU.mult)
    C.act(sn, th, AF.Sin)
    C.act(cs, th, AF.Sin, bias=C.cb[:, 1:2])
    ar = v[:, 10:12]; ai = v[:, 12:14]
    q1 = v[:, 30:32]; q2 = v[:, 32:34]
    for _ in range(4):
        C.tt('dve', q1, cs, cs, ALU.mult); C.tt('dve', q2, sn, sn, ALU.mult)
        C.tt('dve', q2, q1, q2, ALU.subtract)
        C.tt('dve', q1, cs, sn, ALU.mult)
        C.ts('dve', sn, q1, 2.0, ALU.mult)
        C.cp('dve', cs, q2)
    C.tt('dve', ar, mag, cs, ALU.mult)
    C.tt('dve', ai, mag, sn, ALU.mult)
    den = v[:, 14:16]; t1 = v[:, 16:18]; t2 = v[:, 18:20]
    C.tt('dve', den, A_RE, A_RE, ALU.mult); C.tt('dve', t1, A_IM, A_IM, ALU.mult); C.tt('dve', den, den, t1, ALU.add)
    C.recip(den, den)
    nr = v[:, 20:22]; C.ts('dve', nr, ar, -1.0, ALU.add)
    cre = v[:, 22:24]; cim = v[:, 24:26]
    C.tt('dve', t1, nr, A_RE, ALU.mult); C.tt('dve', t2, ai, A_IM, ALU.mult); C.tt('dve', t1, t1, t2, ALU.add); C.tt('dve', cre, t1, den, ALU.mult)
    C.tt('dve', t1, ai, A_RE, ALU.mult); C.tt('dve', t2, nr, A_IM, ALU.mult); C.tt('dve', t1, t1, t2, ALU.subtract); C.tt('dve', cim, t1, den, ALU.mult)
    ncim = v[:, 26:28]; C.ts('dve', ncim, cim, -1.0, ALU.mult)
    pwr = sb("s5pwr", [128, 2, NL + 1]); pwi = sb("s5pwi", [128, 2, NL + 1]); pwn = sb("s5pwn", [128, 2, NL + 1])
    C.cp('dve', pwr[:, :, 0], ar); C.cp('dve', pwi[:, :, 0], ai)
    for k in range(NL):
        C.tt('dve', t1, pwr[:, :, k], pwr[:, :, k], ALU.mult)
        C.tt('dve', t2, pwi[:, :, k], pwi[:, :, k], ALU.mult)
        C.tt('dve', pwr[:, :, k + 1], t1, t2, ALU.subtract)
        C.tt('dve', t1, pwr[:, :, k], pwi[:, :, k], ALU.mult)
        C.ts('dve', pwi[:, :, k + 1], t1, 2.0, ALU.mult)
    C.ts('dve', pwn[:], pwi[:], -1.0, ALU.mult)
    u32 = [sb("s5u32_%d" % i, [64, TB]) for i in range(2)]
    ub = [sb("s5ub_%d" % i, [64, TB], BF16) for i in range(2)]
    xr = [sb("s5xr%d" % i, [128, TB]) for i in range(2)]
    xi = [sb("s5xi%d" % i, [128, TB]) for i in range(2)]
    tm = [sb("s5tm%d" % i, [128, TB]) for i in range(2)]
    sbf = [[sb("s5sb%d_%d" % (j, c), [128, TB], BF16) for c in range(2)] for j in range(2)]
    bu = sb("s5bu", [128, 2, TB])
    carry = sb("s5carry", [128, 4]); C.memset('pool', carry[:], 0.0)
    cz = sb("s5cz", [128, 4])
    yo = [sb("s5yo%d" % i, [64, 512]) for i in range(2)]
    gel = [sb("s5gel%d" % i, [64, 512]) for i in range(2)]
    nblk = T8 // TB
    for b in range(nblk):
        t0 = b * TB
        uu = u32[b % 2]; ubb = ub[b % 2]
        C.dma(uu[:], zs_d[:, t0:t0 + TB])
        C.dma(ubb[:], zs_d[:, t0:t0 + TB], q='pool')
        for j in range(2):
            for c0 in range(0, TB, 256):
                n = min(256, TB - c0)
                p1 = C.psb[5][:, 0:256]; p2 = C.psb[5][:, 256:512]
                C.mm(p1[:, :n], bw[:, j * 128:(j + 1) * 128], ubb[:, c0:c0 + n])
                C.mm(p2[:, :n], bw[:, 256 + j * 128:256 + (j + 1) * 128], ubb[:, c0:c0 + n])
                C.cp('act', bu[:, 0, c0:c0 + n], p1[:, :n])
                C.cp('act', bu[:, 1, c0:c0 + n], p2[:, :n])
                C.ts('dve', tm[0][:, c0:c0 + n], bu[:, 1, c0:c0 + n], ncim[:, j:j + 1], ALU.mult)
                C.stt('dve', xr[0][:, c0:c0 + n], bu[:, 0, c0:c0 + n], cre[:, j:j + 1], tm[0][:, c0:c0 + n], ALU.mult, ALU.add)
                C.ts('dve', tm[1][:, c0:c0 + n], bu[:, 0, c0:c0 + n], cim[:, j:j + 1], ALU.mult)
                C.stt('dve', xi[0][:, c0:c0 + n], bu[:, 1, c0:c0 + n], cre[:, j:j + 1], tm[1][:, c0:c0 + n], ALU.mult, ALU.add)
            yield
            cr = carry[:, j:j + 1]; ci = carry[:, 2 + j:3 + j]
            C.tt('dve', cz[:, 0:1], cr, ar[:, j:j + 1], ALU.mult)
            C.tt('dve', cz[:, 1:2], ci, ai[:, j:j + 1], ALU.mult)
            C.tt('dve', cz[:, 0:1], cz[:, 0:1], cz[:, 1:2], ALU.subtract)
            C.tt('dve', xr[0][:, 0:1], xr[0][:, 0:1], cz[:, 0:1], ALU.add)
            C.tt('dve', cz[:, 2:3], cr, ai[:, j:j + 1], ALU.mult)
            C.tt('dve', cz[:, 3:4], ci, ar[:, j:j + 1], ALU.mult)
            C.tt('dve', cz[:, 2:3], cz[:, 2:3], cz[:, 3:4], ALU.add)
            C.tt('dve', xi[0][:, 0:1], xi[0][:, 0:1], cz[:, 2:3], ALU.add)
            cur = 0
            for k in range(NL):
                d = 1 << k
                a, bb = cur, 1 - cur
                n = TB - d
                C.stt('dve', tm[0][:, 0:n], xi[a][:, 0:n], pwn[:, j, k:k + 1], xr[a][:, d:TB], ALU.mult, ALU.add)
                C.stt('dve', xr[bb][:, d:TB], xr[a][:, 0:n], pwr[:, j, k:k + 1], tm[0][:, 0:n], ALU.mult, ALU.add)
                C.stt('dve', tm[1][:, 0:n], xr[a][:, 0:n], pwi[:, j, k:k + 1], xi[a][:, d:TB], ALU.mult, ALU.add)
                C.stt('dve', xi[bb][:, d:TB], xi[a][:, 0:n], pwr[:, j, k:k + 1], tm[1][:, 0:n], ALU.mult, ALU.add)
                C.cp('pool', xr[bb][:, 0:d], xr[a][:, 0:d])
                C.cp('pool', xi[bb][:, 0:d], xi[a][:, 0:d])
                cur = bb
                yield
            C.cp('pool', carry[:, j:j + 1], xr[cur][:, TB - 1:TB])
            C.cp('pool', carry[:, 2 + j:3 + j], xi[cur][:, TB - 1:TB])
            C.cp('act', sbf[j][0][:], xr[cur][:])
            C.cp('act', sbf[j][1][:], xi[cur][:])
        for c0 in range(0, TB, 256):
            n = min(256, TB - c0)
            p = C.psb[5][:, 0:256]
            k = 0
            for j in range(2):
                for c in range(2):
                    C.mm(p[0:64, :n], cw[:, c * 128 + j * 64:c * 128 + (j + 1) * 64], sbf[j][c][:, c0:c0 + n], start=(k == 0), stop=(k == 3))
                    k += 1
            y = yo[(c0 // 256) % 2]
            C.stt('dve', y[:, :n], uu[:, c0:c0 + n], dsk[:, 0:1], p[0:64, :n], ALU.mult, ALU.add)
            g_ = gel[(c0 // 256) % 2]
            C.tt('pool', g_[:, :n], y[:, :n], y[:, :n], ALU.mult)
            C.ts('dve', g_[:, :n], g_[:, :n], 0.044715, ALU.mult, 1.0, ALU.add)
            C.tt('pool', g_[:, :n], g_[:, :n], y[:, :n], ALU.mult)
            C.act(g_[:, :n], g_[:, :n], AF.Tanh, scale=0.7978845608028654)
            C.stt('dve', g_[:, :n], g_[:, :n], 1.0, y[:, :n], ALU.add, ALU.mult)
            C.ts('dve', y[:, :n], g_[:, :n], 0.5, ALU.mult)
            C.dma(ys_d[:, t0 + c0:t0 + c0 + n], y[:, :n])

GN_EPS = 64e-5


def build_RWKV(C, T8, zr_d, rp_d, rw2_d, rg2_d, rln_d, yr_d):
    sb = C.sb
    TB = min(256, T8)
    NC = TB // 64
    rp = sb("rp", [128, 16]); C.dma(rp[:], rp_d)
    hb = sb("rhb", [128, 2]); C.ts('dve', hb[:], rp[:, 7:9], 0.5, ALU.mult)
    rw2 = sb("rw2", [96, 128]); C.dma(rw2[:], rw2_d)
    rg2 = sb("rg2", [128, 128]); C.dma(rg2[:], rg2_d)
    rln = sb("rln", [64, 128]); C.dma(rln[:], rln_d)
    m_ui = sb("m_ui", [64, NC, 64]); m_us = sb("m_us", [64, NC, 64]); m_ls = sb("m_ls", [64, NC, 64])
    for mt, op, st, cm in ((m_ui, ALU.is_ge, 1, -1), (m_us, ALU.is_gt, 1, -1), (m_ls, ALU.is_gt, -1, 1)):
        C.memset('pool', mt[:], 1.0)
        C.asel(mt[:], mt[:], [[0, NC], [st, 64]], op, 0.0, base=0, cm=cm)
    identc = sb("identc", [64, NC, 64])
    C.memset('pool', identc[:], 1.0)
    C.asel(identc[:], identc[:], [[0, NC], [1, 64]], ALU.is_equal, 0.0, base=0, cm=-1)
    epsg = sb("epsg", [64, 1]); C.memset('pool', epsg[:], GN_EPS)
    M = sb("rM", [64, 64]); C.memset('pool', M[:], 0.0)
    pieces = [(0, 64), (64, 64), (128, 64), (192, 96), (288, 96), (384, 128), (512, 128)]
    raw = [sb("rraw%d" % i, [n, TB + 1]) for i, (o, n) in enumerate(pieces)]
    prevcol = [sb("rprev%d" % i, [n, 1]) for i, (o, n) in enumerate(pieces)]
    sh = [sb("rsh%d" % i, [n, TB]) for i, (o, n) in enumerate(pieces)]
    tmpa = sb("rtmpa", [128, TB])
    f = lambda name, p=64: sb(name, [p, TB])
    lw = f("r_lw"); aa = f("r_a"); kk = f("r_kk"); km = f("r_km"); cum = f("r_cum")
    pp = f("r_p"); pinv = f("r_pinv"); pm1 = f("r_pm1")
    rt = f("r_rt"); kt_ = f("r_kt"); at = f("r_at"); bt = f("r_bt"); t64 = f("r_t64"); t64b = f("r_t64b")
    sgl = sb("r_sgl", [128, 2, TB])
    ones64 = sb("ones64c", [64, TB]); C.memset('pool', ones64[:], 1.0)
    vtok = sb("r_vtok", [64, NC, 64]); atok = sb("r_atok", [64, NC, 64]); ktok = sb("r_ktok", [64, NC, 64])
    gN = sb("r_N", [64, NC, 64]); gNT = sb("r_NT", [64, NC, 64]); gN2 = sb("r_N2", [64, NC, 64]); gNT2 = sb("r_NT2", [64, NC, 64])
    WT = sb("r_WT", [64, NC, 64]); WT2 = sb("r_WT2", [64, NC, 64])
    AakT = sb("r_AakT", [64, NC, 64]); AraT = sb("r_AraT", [64, NC, 64]); ArkT = sb("r_ArkT", [64, NC, 64])
    gtok = sb("r_gtok", [64, NC, 64]); bsum = sb("r_bsum", [64, NC])
    ytok = sb("r_ytok", [64, NC, 64]); X0 = sb("r_X0", [64, 64]); U = sb("r_U", [64, 64]); MpL = sb("r_MpL", [64, 64])
    st8 = sb("r_st8", [64, 4 * NC]); yo = sb("r_yo", [64, NC, 64])
    nblk = T8 // TB
    for b in range(nblk):
        t0 = b * TB
        for i, (o, n) in enumerate(pieces):
            if b == 0:
                C.memset('pool', raw[i][:, 0:1], 0.0)
            else:
                C.cp('pool', raw[i][:, 0:1], prevcol[i][:])
            C.dma(raw[i][:, 1:TB + 1], zr_d[o:o + n, t0:t0 + TB])
            C.cp('pool', prevcol[i][:], raw[i][:, TB:TB + 1])
            C.tt('pool', tmpa[:n, :], raw[i][:, 0:TB], raw[i][:, 1:TB + 1], ALU.subtract)
            C.stt('dve', sh[i][:], tmpa[:n, :], rp[:n, i:i + 1], raw[i][:, 1:TB + 1], ALU.mult, ALU.add)
        r_, k_, v_, wl_, al_, gl0, gl1 = sh
        C.act(wl_[:], wl_[:], AF.Tanh)
        ps = C.ps(6, 8)
        C.mm(ps[0:64, :TB], rw2[:, 0:64], wl_[:])
        C.act(lw[:], ps[0:64, :TB], AF.Tanh, bias=hb[0:64, 0:1], scale=0.5)
        C.ts('dve', lw[:], lw[:], -0.3032653298563167, ALU.mult, -0.3032653298563167, ALU.add)
        ps = C.ps(6, 8)
        C.mm(ps[0:64, :TB], rw2[:, 64:128], al_[:])
        C.act(aa[:], ps[0:64, :TB], AF.Tanh, bias=hb[0:64, 1:2], scale=0.5)
        C.ts('dve', aa[:], aa[:], 0.5, ALU.mult, 0.5, ALU.add)
        C.act(sgl[:, 0, :], gl0[:], AF.Tanh, scale=0.5)
        C.act(sgl[:, 1, :], gl1[:], AF.Tanh, scale=0.5)
        C.ts('dve', sgl[:], sgl[:], 0.5, ALU.mult, 0.5, ALU.add)
        psg = C.ps(6, 8)
        for c in range(NC):
            for h2 in range(2):
                C.mm(psg[0:64, c * 64:(c + 1) * 64], sgl[:, h2, c * 64:(c + 1) * 64], rg2[:, h2 * 64:(h2 + 1) * 64], start=(h2 == 0), stop=(h2 == 1))
        C.cp('act', gtok[:], psg[0:64, 0:NC * 64].rearrange("p (c v) -> p c v", c=NC))
        yield
        C.ts('dve', kk[:], k_[:], rp[0:64, 9:10], ALU.mult)
        C.tt('pool', t64[:], kk[:], kk[:], ALU.mult)
        ps = C.ps(6, 8)
        C.mm(ps[0:64, :TB], C.ones[0:64, 0:64], t64[:])
        C.act(t64[:], ps[0:64, :TB], AF.Sqrt)
        C.ts('dve', t64[:], t64[:], 1e-12, ALU.max)
        C.recip(t64[:], t64[:])
        C.tt('dve', kk[:], kk[:], t64[:], ALU.mult)
        C.ts('dve', t64[:], aa[:], -1.0, ALU.add, rp[0:64, 10:11], ALU.mult)
        C.ts('dve', t64[:], t64[:], 1.0, ALU.add)
        C.tt('dve', km[:], k_[:], t64[:], ALU.mult)
        yield
        for c in range(NC):
            C.scan(cum[:, c * 64:(c + 1) * 64], ones64[:, 0:64], lw[:, c * 64:(c + 1) * 64], 0.0)
        C.act(pp[:], cum[:], AF.Exp)
        C.act(pinv[:], cum[:], AF.Exp, scale=-1.0)
        C.tt('pool', t64[:], cum[:], lw[:], ALU.subtract)
        C.act(pm1[:], t64[:], AF.Exp)
        C.tt('dve', rt[:], r_[:], pp[:], ALU.mult)
        C.tt('dve', kt_[:], km[:], pinv[:], ALU.mult)
        C.tt('dve', t64[:], kk[:], aa[:], ALU.mult)
        C.stt('dve', at[:], t64[:], -1.0, pinv[:], ALU.mult, ALU.mult)
        C.tt('dve', bt[:], kk[:], pm1[:], ALU.mult)
        C.stt('dve', t64b[:], r_[:], rp[0:64, 11:12], km[:], ALU.mult, ALU.mult)
        psb_ = C.ps(6, 8)
        for c in range(NC):
            C.mm(psb_[0:64, c:c + 1], t64b[:, c * 64:(c + 1) * 64], C.ones[0:64, 0:1])
        C.cp('dve', bsum[:], psb_[0:64, 0:NC])
        yield
        for src, dst in ((v_, vtok), (at, atok), (kt_, ktok)):
            pt = C.ps(6, 8)
            for c in range(NC):
                C.tr(pt[0:64, c * 64:(c + 1) * 64], src[:, c * 64:(c + 1) * 64], C.ident[0:64, 0:64])
            C.evac(dst[:], pt[0:64, 0:NC * 64].rearrange("p (c v) -> p c v", c=NC))
        def gram(dst, lhs, rhs, mask):
            pg = C.ps(6, 8)
            for c in range(NC):
                C.mm(pg[0:64, c * 64:(c + 1) * 64], lhs[:, c * 64:(c + 1) * 64], rhs[:, c * 64:(c + 1) * 64])
            C.tt('dve', dst[:], pg[0:64, 0:NC * 64].rearrange("p (c v) -> p c v", c=NC), mask[:], ALU.mult)
        gram(gN, bt, at, m_ls)
        gram(gNT, at, bt, m_us)
        gram(AakT, kt_, bt, m_us)
        gram(AraT, at, rt, m_ui)
        gram(ArkT, kt_, rt, m_ui)
        yield
        C.tt('dve', WT[:], gNT[:], identc[:], ALU.add)
        Nk, NkT, Nn, NnT = gN, gNT, gN2, gNT2
        Wc, Wn = WT, WT2
        for lev in range(5):
            p1 = C.ps(6, 8); p2 = C.ps(6, 8)
            for c in range(NC):
                C.mm(p1[0:64, c * 64:(c + 1) * 64], NkT[:, c, :], Nk[:, c, :])
                C.mm(p2[0:64, c * 64:(c + 1) * 64], Nk[:, c, :], NkT[:, c, :])
            C.cp('dve', Nn[:], p1[0:64, 0:NC * 64].rearrange("p (c v) -> p c v", c=NC))
            C.cp('act', NnT[:], p2[0:64, 0:NC * 64].rearrange("p (c v) -> p c v", c=NC))
            p3 = C.ps(6, 8)
            for c in range(NC):
                C.mm(p3[0:64, c * 64:(c + 1) * 64], Nn[:, c, :], Wc[:, c, :])
            C.tt('dve', Wn[:], p3[0:64, 0:NC * 64].rearrange("p (c v) -> p c v", c=NC), Wc[:], ALU.add)
            Nk, NkT, Nn, NnT = Nn, NnT, Nk, NkT
            Wc, Wn = Wn, Wc
            yield
        for c in range(NC):
            cs = slice(c * 64, (c + 1) * 64)
            px = C.ps(6, 8)
            C.mm(px[0:64, 0:64], bt[:, cs], M[:], start=True, stop=False)
            C.mm(px[0:64, 0:64], AakT[:, c, :], vtok[:, c, :], start=False, stop=True)
            C.cp('dve', X0[:], px[0:64, 0:64])
            yield
            C.ts('dve', MpL[:], M[:], pp[:, c * 64 + 63:c * 64 + 64], ALU.mult)
            pu = C.ps(6, 8)
            C.mm(pu[0:64, 0:64], Wc[:, c, :], X0[:])
            C.cp('dve', U[:], pu[0:64, 0:64])
            yield
            py = C.ps(6, 8)
            C.mm(py[0:64, 0:64], rt[:, cs], M[:], start=True, stop=False)
            C.mm(py[0:64, 0:64], AraT[:, c, :], U[:], start=False, stop=False)
            C.mm(py[0:64, 0:64], ArkT[:, c, :], vtok[:, c, :], start=False, stop=True)
            pm = C.ps(6, 8)
            C.mm(pm[0:64, 0:64], atok[:, c, :], U[:], start=True, stop=False)
            C.mm(pm[0:64, 0:64], ktok[:, c, :], vtok[:, c, :], start=False, stop=True)
            C.cp('act', ytok[:, c, :], py[0:64, 0:64])
            C.stt('dve', M[:], pm[0:64, 0:64], pp[:, c * 64 + 63:c * 64 + 64], MpL[:], ALU.mult, ALU.add)
            yield
        C.memset('pool', st8[:], 0.0)
        C.P.op('dve', lambda E: E.tensor_reduce(st8[:, 0:NC], ytok[:], AX.X, ALU.add), reads=[ytok[:]], writes=[st8[:, 0:NC]])
        C.ts('dve', st8[:, NC:2 * NC], st8[:, 0:NC], -1.0 / 64, ALU.mult)
        for c in range(NC):
            C.ts('dve', ytok[:, c, :], ytok[:, c, :], st8[:, NC + c:NC + c + 1], ALU.add)
        C.tt('pool', yo[:], ytok[:], ytok[:], ALU.mult)
        C.P.op('dve', lambda E: E.tensor_reduce(st8[:, 2 * NC:3 * NC], yo[:], AX.X, ALU.add), reads=[yo[:]], writes=[st8[:, 2 * NC:3 * NC]])
        C.act(st8[:, 3 * NC:4 * NC], st8[:, 2 * NC:3 * NC], AF.Sqrt, bias=epsg[:, 0:1], scale=1.0 / 64)
        C.recip(st8[:, 3 * NC:4 * NC], st8[:, 3 * NC:4 * NC])
        for c in range(NC):
            C.stt('dve', yo[:, c, :], ytok[:, c, :], st8[:, 3 * NC + c:3 * NC + c + 1], rln[:, 0:64], ALU.mult, ALU.mult)
            C.tt('pool', yo[:, c, :], yo[:, c, :], rln[:, 64:128], ALU.add)
            C.stt('dve', yo[:, c, :], vtok[:, c, :], bsum[:, c:c + 1], yo[:, c, :], ALU.mult, ALU.add)
            C.tt('pool', yo[:, c, :], yo[:, c, :], gtok[:, c, :], ALU.mult)
        C.dma(yr_d[t0:t0 + TB, :].rearrange("(c t) v -> t c v", c=NC), yo[:])
        yield

LN16 = 2.772588722239781


def build_B(T8=8192):
    nc = bass.Bass("TRN2", target_bir_lowering=False)
    di = lambda n, s: nc.dram_tensor(n, s, F32, kind="ExternalInput").ap()
    do = lambda n, s: nc.dram_tensor(n, s, F32, kind="ExternalOutput").ap()
    zqk_d = di("zqk", [512, T8]); convw_d = di("convw", [128, 16]); vtm_d = di("vtm", [T8, 128])
    zi_d = di("zi", [1, T8]); zf_d = di("zf", [1, T8]); gb_d = di("gb", [64, 2])
    hm_d = do("hm", [T8, 128])
    zs_d = di("zs", [64, T8]); s5p_d = di("s5p", [128, 6]); s5b_d = di("s5b", [64, 512]); s5c_d = di("s5c", [128, 256])
    s5d_d = di("s5d", [64, 1]); ys_d = do("ys", [64, T8])
    zr_d = di("zr", [640, T8]); rp_d = di("rp", [128, 16]); rw2_d = di("rw2", [96, 128]); rg2_d = di("rg2", [128, 128])
    rln_d = di("rln", [64, 128]); yr_d = do("yr", [T8, 64])
    C = Ctx(nc)
    C.wrr = 0
    C.act_copy_ok = False
    make_consts(C)
    sb = C.sb
    NT = T8 // 128
    convw = sb("convw", [128, 16]); C.dma(convw[:], convw_d)
    gb = sb("gb", [64, 2]); C.dma(gb[:], gb_d)
    gb15 = sb("gb15", [64, 2]); C.ts('dve', gb15[:], gb[:], 1.0 / 15, ALU.mult)
    qkT = sb("qkT", [128, 4, T8], BF16)
    vaug = sb("vaug", [128, NT, 129], BF16)
    C.memset('pool', vaug[:, :, 128:129], 1.0)
    C.dma(vaug[:, :, 0:128], vtm_d.rearrange("(n p) d -> p n d", p=128), q='pool')
    CH = 1024 if T8 >= 1024 else T8
    cbuf = [sb("cbuf%d" % i, [128, CH + 3]) for i in range(2)]
    cacc = [sb("cacc%d" % i, [128, CH]) for i in range(2)]
    bi = 0
    for tl in range(4):
        for t0 in range(0, T8, CH):
            cb = cbuf[bi % 2]; ca = cacc[bi % 2]; bi += 1
            if t0 == 0:
                C.memset('pool', cb[:, 0:3], 0.0)
                C.dma(cb[:, 3:3 + CH], zqk_d[tl * 128:(tl + 1) * 128, 0:CH])
            else:
                C.dma(cb[:, 0:3 + CH], zqk_d[tl * 128:(tl + 1) * 128, t0 - 3:t0 + CH])
            C.ts('dve', ca[:], cb[:, 3:3 + CH], convw[:, tl * 4:tl * 4 + 1], ALU.mult)
            for j in range(1, 4):
                C.stt('dve', ca[:], cb[:, 3 - j:3 - j + CH], convw[:, tl * 4 + j:tl * 4 + j + 1], ca[:], ALU.mult, ALU.add)
            C.act(qkT[:, tl, t0:t0 + CH], ca[:], AF.Silu)
    nbscr = nc.dram_tensor("nbscr", [T8], F32)
    zi = sb("zi", [NT, 128]); zf = sb("zf", [NT, 128])
    C.dma(zi[:], zi_d.rearrange("o (p j) -> (o p) j", j=128)); C.dma(zf[:], zf_d.rearrange("o (p j) -> (o p) j", j=128))
    C.act(zi[:], zi[:], AF.Tanh, bias=gb15[0:NT, 0:1], scale=1.0 / 15)
    C.act(zf[:], zf[:], AF.Tanh, bias=gb15[0:NT, 1:2], scale=1.0 / 15)
    C.act(zf[:], zf[:], AF.Exp, scale=-15.0)
    C.act(zf[:], zf[:], AF.Ln, bias=C.cb[0:NT, 0:1])
    nb = sb("nb", [NT, 128])
    C.scan(nb[:], C.ones[0:NT, 0:128], zf[:], 0.0)
    ustr = sb("ustr", [64, 64]); C.memset('pool', ustr[:], 1.0)
    C.asel(ustr[:], ustr[:], [[1, 64]], ALU.is_gt, 0.0, base=0, cm=-1)
    tot = sb("gtot", [NT, 1]); C.cp('dve', tot[:], nb[:, 127:128])
    pso = C.psb[0]
    C.mm(pso[0:NT, 0:1], ustr[0:NT, 0:NT], tot[:])
    offs = sb("goffs", [NT, 1]); C.cp('dve', offs[:], pso[0:NT, 0:1])
    C.ts('dve', nb[:], nb[:], offs[:, 0:1], ALU.add)
    C.dma(nbscr.ap().rearrange("(p j) -> p j", j=128), nb[:])
    crow = sb("crow", [NT, 128])
    C.stt('dve', crow[:], zi[:], 15.0, nb[:], ALU.mult, ALU.add)
    C.ts('dve', crow[:], crow[:], -LN16, ALU.add)
    cT = sb("cT", [128, NT])
    pst = C.psb[0]
    C.tr(pst[:, 0:NT], crow[:], C.ident[0:NT, 0:NT])
    C.cp('dve', cT[:], pst[:, 0:NT])
    nbrow = nbscr.ap().rearrange("(o t) -> o t", o=1)
    bBc = [sb("bBc%d" % i, [128, 512]) for i in range(2)]
    negtri = sb("negtri", [128, 128]); C.memset('pool', negtri[:], 0.0)
    C.asel(negtri[:], negtri[:], [[1, 128]], ALU.is_ge, -30000.0, base=0, cm=-1)
    dtb = [sb("dtb%d" % i, [128, 512]) for i in range(2)]
    ptb = [sb("ptb%d" % i, [128, 512], BF16) for i in range(2)]
    dgt = [sb("dgt%d" % i, [128, 128]) for i in range(2)]
    hmo = [sb("hmo%d" % i, [128, 128]) for i in range(2)]
    den = sb("mden", [128, 8])
    den2 = sb("mden2", [128, 8])
    it = 0
    NQ = min(512, T8)
    QT = NQ // 128
    def mlstm_main():
      nonlocal it
      for ch in range(T8 // NQ):
          q0 = ch * NQ
          accs = [C.psb[1 + j] for j in range(QT)]
          bB = bBc[ch % 2]
          C.dma(bB[:, 0:NQ], nbrow[:, q0:q0 + NQ].to_broadcast([128, NQ]))
          for ks in range(QT * ch + QT):
              m = max(0, ks - QT * ch)
              lo = m * 128
              ps = C.psb[0]
              for kd in range(2):
                  C.mm(ps[:, lo:NQ], qkT[:, 2 + kd, ks * 128:(ks + 1) * 128], qkT[:, kd, q0 + lo:q0 + NQ], start=(kd == 0), stop=(kd == 1))
              dt = dtb[it % 2]; pt = ptb[it % 2]; dg = dgt[it % 2]; it += 1
              if ks >= QT * ch:
                  C.tt('pool', dg[:], negtri[:], bB[:, lo:lo + 128], ALU.subtract)
                  C.act(dt[:, lo:lo + 128], dg[:], AF.Exp, bias=cT[:, ks:ks + 1])
                  if lo + 128 < NQ:
                      C.act(dt[:, lo + 128:NQ], bB[:, lo + 128:NQ], AF.Exp, bias=cT[:, ks:ks + 1], scale=-1.0)
              else:
                  C.act(dt[:, lo:NQ], bB[:, lo:NQ], AF.Exp, bias=cT[:, ks:ks + 1], scale=-1.0)
              C.tt('dve', pt[:, lo:NQ], ps[:, lo:NQ], dt[:, lo:NQ], ALU.mult)
              for j in range(m, QT):
                  C.mm(accs[j][:, 0:129], pt[:, j * 128:(j + 1) * 128], vaug[:, ks, :], start=(ks == 0), stop=(ks == QT * ch + j))
              yield
          for j in range(QT):
              d = den[:, j:j + 1]
              C.ts('dve', den2[:, j:j + 1], accs[j][:, 128:129], -1.0, ALU.mult)
              C.cp('dve', d, accs[j][:, 128:129])
              C.tt('dve', d, d, den2[:, j:j + 1], ALU.max)
              C.ts('dve', d, d, 1.0, ALU.max)
              C.recip(d, d)
              ho = hmo[j % 2]
              C.ts('dve', ho[:], accs[j][:, 0:128], d, ALU.mult)
              C.dma(hm_d[q0 + j * 128:q0 + (j + 1) * 128, :], ho[:])
    gens = [mlstm_main(), build_S5(C, T8, zs_d, s5p_d, s5b_d, s5c_d, s5d_d, ys_d), build_RWKV(C, T8, zr_d, rp_d, rw2_d, rg2_d, rln_d, yr_d)]
    while gens:
        for g in list(gens):
            try:
                next(g)
            except StopIteration:
                gens.remove(g)
    nw = C.finish([hm_d, ys_d, yr_d])
    return nc, nw

M_IN = 4104
RWKV_IN = 1984
R0 = M_IN
S0 = M_IN + RWKV_IN
G0 = S0 + 512
NIN = 12744


def b_inputs(z, inp, l, c):
    f = np.float32
    hd, half = c // 2, c % 2
    T = z.shape[0]
    A = np.ascontiguousarray
    d = {}
    d["zqk"] = A(np.concatenate([z[:, hd * 256:(hd + 1) * 256].T, z[:, 1024 + hd * 256:1024 + (hd + 1) * 256].T], 0))
    cw = np.zeros((128, 16), f)
    conv = inp["mlstm_conv"][l]
    for tl in range(4):
        base = (hd * 256 + tl * 128) if tl < 2 else (1024 + hd * 256 + (tl - 2) * 128)
        for j in range(4):
            cw[:, tl * 4 + j] = conv[j, base:base + 128]
    d["convw"] = cw
    vc = 2048 + hd * 256 + half * 128
    d["vtm"] = A(z[:, vc:vc + 128])
    d["zi"] = A(z[:, 4096 + hd][None, :])
    d["zf"] = A(z[:, 4100 + hd][None, :])
    d["gb"] = np.tile(np.array([[inp["mlstm_ib"][l, hd], inp["mlstm_fb"][l, hd]]], f), (64, 1))
    d["zs"] = A(z[:, S0 + c * 64:S0 + (c + 1) * 64].T)
    s5p = np.zeros((128, 6), f)
    s5b = np.zeros((64, 512), f)
    s5c = np.zeros((128, 256), f)
    for j in range(2):
        for g2 in range(2):
            gl = 2 * j + g2
            g = 4 * c + gl
            rows = slice(g2 * 64, (g2 + 1) * 64)
            s5p[rows, 0 + j] = inp["s5_a_re"][l, g]
            s5p[rows, 2 + j] = inp["s5_a_im"][l, g]
            s5p[rows, 4 + j] = inp["s5_log_dt"][l, g]
            s5b[gl * 16:(gl + 1) * 16, j * 128 + g2 * 64:j * 128 + (g2 + 1) * 64] = inp["s5_b_re"][l, g].T
            s5b[gl * 16:(gl + 1) * 16, 256 + j * 128 + g2 * 64:256 + j * 128 + (g2 + 1) * 64] = inp["s5_b_im"][l, g].T
            s5c[rows, j * 64 + gl * 16:j * 64 + (gl + 1) * 16] = inp["s5_c_re"][l, g].T
            s5c[rows, 128 + j * 64 + gl * 16:128 + j * 64 + (gl + 1) * 16] = inp["s5_c_im"][l, g].T
    d["s5p"], d["s5b"], d["s5c"] = s5p, s5b, s5c
    d["s5d"] = A(inp["s5_d"][l, c * 64:(c + 1) * 64][:, None])
    hc = slice(c * 64, (c + 1) * 64)
    cols = [(R0 + c * 64, 64), (R0 + 512 + c * 64, 64), (R0 + 1024 + c * 64, 64), (R0 + 1536, 96), (R0 + 1632, 96),
            (R0 + 1728, 128), (R0 + 1856, 128)]
    d["zr"] = A(np.concatenate([z[:, o:o + n].T for o, n in cols], 0))
    rp = np.zeros((128, 16), f)
    mu = inp["rwkv_mu"][l]
    for i, (o, n) in enumerate(cols):
        rp[:n, i] = mu[o - R0:o - R0 + n]
    for k, name in ((7, "rwkv_w0"), (8, "rwkv_a0"), (9, "rwkv_kk"), (10, "rwkv_ka"), (11, "rwkv_rk")):
        rp[:64, k] = inp[name][l, hc]
    d["rp"] = rp
    d["rw2"] = A(np.concatenate([inp["rwkv_w2"][l][:, hc], inp["rwkv_a2"][l][:, hc]], 1))
    g2w = inp["rwkv_g2"][l][:, hc]
    d["rg2"] = A(np.concatenate([g2w[0:128], g2w[128:256]], 1))
    d["rln"] = A(np.concatenate([np.tile(inp["rwkv_ln_g"][l, hc][None, :], (64, 1)), np.tile(inp["rwkv_ln_b"][l, hc][None, :], (64, 1))], 1))
    return {k: A(v.astype(f)) for k, v in d.items()}

def dense_multi(C, streams, col0, ncols, T, epi, gcols=128):
    c = col0
    while c < col0 + ncols:
        gw = min(gcols, col0 + ncols - c)
        views = []
        for (w2d, K, rhs, wbufs) in streams:
            KT = K // 128
            wb = wbufs[C.wrr % len(wbufs)]
            wbv = wb[:, 0:KT * gw].rearrange("p (kt n) -> p kt n", kt=KT)
            C.dma(wbv, w2d.rearrange("(kt p) n -> p kt n", p=128)[:, :, c:c + gw], q='pool')
            views.append(wbv)
        C.wrr += 1
        for cc in range(0, gw, 128):
            cw = min(128, gw - cc)
            banks = []
            for si, (w2d, K, rhs, wbufs) in enumerate(streams):
                KT = K // 128
                ps = C.psb[(C.psi % 2) * 3 + si]
                for kt in range(KT):
                    C.mm(ps[:cw, :T], views[si][:, kt, cc:cc + cw], rhs(kt)[:, 0:T], start=(kt == 0), stop=(kt == KT - 1))
                banks.append(ps)
            C.psi += 1
            epi(c + cc, cw, banks)
        c += gw


def build_C(final=False, TH=512, NH=2):
    nc = bass.Bass("TRN2", target_bir_lowering=False)
    D = 2048; KT = 16; FF = 5632
    T = TH * NH
    di = lambda n, s: nc.dram_tensor(n, s, F32, kind="ExternalInput").ap()
    hT_d = di("hT", [D, T]); hm_d = di("hmT", [1024, T]); zo_d = di("zoT", [1024, T]); yr_d = di("yrT", [512, T])
    ys_d = di("ysT", [512, T]); zg_d = di("zgT", [6144, T]); p_d = di("pT", [256, T])
    vec_d = di("vecs", [128, 80])
    glu_d = di("glu_w", [512, 512]); upm_d = di("w_up_m", [1024, D]); upr_d = di("w_up_r", [512, D]); ups_d = di("w_up_s", [512, D])
    wout_d = di("w_out", [D, D]); wg_d = di("ffn_w_gate", [D, FF]); wu_d = di("ffn_w_up", [D, FF]); wd_d = di("ffn_w_down", [FF, D])
    pg_d = di("ple_w_gate", [D, D]); pp_d = di("ple_w_proj", [256, D])
    out_d = nc.dram_tensor("hout", [D, T], F32, kind="ExternalOutput").ap()
    C = Ctx(nc)
    C.wrr = 0
    make_consts(C)
    C.epsb = C.cb[:, 2:3]
    sb = C.sb
    vec = sb("vecs", [128, 80]); C.dma(vec[:], vec_d)
    hT = sb("hT", [128, KT, TH])
    xn = sb("xn", [128, KT, TH], BF16)
    sq = sb("sq", [128, 2, TH]); rstd = sb("rstd", [128, TH])
    ym = sb("ym", [128, 8, TH], BF16); yr = sb("yr", [128, 4, TH], BF16); ysb = sb("ysb", [128, 4, TH], BF16)
    ys32 = sb("ys32", [128, 4, TH])
    mixed = sb("mixed", [128, KT, TH], BF16)
    hid = sb("hid", [128, 44, TH], BF16)
    pb = sb("pb", [128, 2, TH], BF16)
    ld = [sb("ld%d" % i, [128, TH]) for i in range(6)]
    wA = [sb("wA%d" % i, [128, 44 * 128], BF16) for i in range(2)]
    wB = [sb("wB%d" % i, [128, 16 * 128], BF16) for i in range(2)]
    wC = [sb("wC%d" % i, [128, 16 * 128], BF16) for i in range(2)]
    li = [0]

    def nld():
        li[0] += 1
        return ld[li[0] % 6]
    for hf in range(NH):
        ts_ = slice(hf * TH, (hf + 1) * TH)
        hv = hT_d.rearrange("(kt p) t -> p kt t", p=128)
        for kt in range(KT):
            C.dma(hT[:, kt, :], hv[:, kt, ts_])
        C.dma(yr[:], yr_d.rearrange("(kt p) t -> p kt t", p=128)[:, :, ts_], q='pool')
        C.dma(ysb[:], ys_d.rearrange("(kt p) t -> p kt t", p=128)[:, :, ts_], q='pool')
        C.dma(ys32[:], ys_d.rearrange("(kt p) t -> p kt t", p=128)[:, :, ts_])
        C.dma(pb[:], p_d.rearrange("(kt p) t -> p kt t", p=128)[:, :, ts_], q='pool')
        for hd in range(4):
            tl = []
            bank = C.psb[6]
            for k2 in range(2):
                t_ = nld(); C.dma(t_[:], hm_d[(hd * 2 + k2) * 128:(hd * 2 + k2 + 1) * 128, ts_]); tl.append(t_)
                s = sq[:, k2, :]
                C.act(s, t_[:], AF.Square)
                C.mm(bank[:, :TH], C.ones[:], s, start=(k2 == 0), stop=(k2 == 1))
            C.act(rstd[:], bank[:, :TH], AF.Sqrt, bias=C.epsb, scale=1.0 / 256)
            C.recip(rstd[:], rstd[:])
            for k2 in range(2):
                ct = hd * 2 + k2
                o_ = nld(); C.dma(o_[:], zo_d[ct * 128:(ct + 1) * 128, ts_])
                C.act(o_[:], o_[:], AF.Sigmoid)
                C.stt('dve', tl[k2][:], tl[k2][:], vec[:, ct:ct + 1], rstd[:], ALU.mult, ALU.mult)
                C.tt('dve', ym[:, ct, :], tl[k2][:], o_[:], ALU.mult)
        def epi_glu(c0, cw, banks):
            ct = c0 // 128
            t_ = nld()
            C.act(t_[:], banks[0][:, :TH], AF.Sigmoid, bias=vec[:, 8 + ct:9 + ct])
            C.tt('dve', ysb[:, ct, :], t_[:], ys32[:, ct, :], ALU.mult)
        ysg = hid[:, 0:4, :]
        def epi_glu2(c0, cw, banks):
            ct = c0 // 128
            t_ = nld()
            C.act(t_[:], banks[0][:, :TH], AF.Sigmoid, bias=vec[:, 8 + ct:9 + ct])
            C.tt('dve', ysg[:, ct, :], t_[:], ys32[:, ct, :], ALU.mult)
        dense_multi(C, [(glu_d, 512, lambda kt: ysb[:, kt, :], wB)], 0, 512, TH, epi_glu2)
        C.cp('pool', ysb[:], ysg)
        def epi_mix(c0, cw, banks):
            ct = c0 // 128
            acc = nld()
            for bi in range(3):
                g_ = nld(); C.dma(g_[:], zg_d[bi * 2048 + c0:bi * 2048 + c0 + 128, ts_])
                C.act(g_[:], g_[:], AF.Sigmoid)
                if bi == 0:
                    C.tt('dve', acc[:], g_[:], banks[bi][:, :TH], ALU.mult)
                else:
                    C.tt('dve', g_[:], g_[:], banks[bi][:, :TH], ALU.mult)
                    C.tt('pool', acc[:], acc[:], g_[:], ALU.add)
            C.cp('act', mixed[:, ct, :], acc[:])
        dense_multi(C, [(upm_d, 1024, lambda kt: ym[:, kt, :], wA), (upr_d, 512, lambda kt: yr[:, kt, :], wB), (ups_d, 512, lambda kt: ysb[:, kt, :], wC)], 0, D, TH, epi_mix)
        def epi_res(c0, cw, banks):
            ct = c0 // 128
            C.tt('dve', hT[:, ct, :], hT[:, ct, :], banks[0][:, :TH], ALU.add)
        dense_multi(C, [(wout_d, D, lambda kt: mixed[:, kt, :], wB)], 0, D, TH, epi_res)
        rmsnorm_fm(C, hT, KT, TH, vec[:, 16:32], xn, sq, rstd)
        def epi_ffn(c0, cw, banks):
            ct = c0 // 128
            t_ = nld()
            C.act(t_[:], banks[0][:, :TH], AF.Silu)
            C.tt('dve', hid[:, ct, :], t_[:], banks[1][:, :TH], ALU.mult)
        dense_multi(C, [(wg_d, D, lambda kt: xn[:, kt, :], wB), (wu_d, D, lambda kt: xn[:, kt, :], wC)], 0, FF, TH, epi_ffn)
        dense_multi(C, [(wd_d, FF, lambda kt: hid[:, kt, :], wA)], 0, D, TH, epi_res)
        rmsnorm_fm(C, hT, KT, TH, vec[:, 32:48], xn, sq, rstd)
        def epi_ple(c0, cw, banks):
            ct = c0 // 128
            t_ = nld()
            C.act(t_[:], banks[0][:, :TH], AF.Sigmoid)
            C.tt('dve', t_[:], t_[:], banks[1][:, :TH], ALU.mult)
            C.tt('pool', hT[:, ct, :], hT[:, ct, :], t_[:], ALU.add)
        dense_multi(C, [(pg_d, D, lambda kt: xn[:, kt, :], wB), (pp_d, 256, lambda kt: pb[:, kt, :], wC)], 0, D, TH, epi_ple)
        ov = out_d.rearrange("(kt p) t -> p kt t", p=128)
        if final:
            rmsnorm_fm(C, hT, KT, TH, vec[:, 48:64], xn, sq, rstd)
            for kt in range(KT):
                t_ = nld()
                C.stt('dve', t_[:], hT[:, kt, :], vec[:, 48 + kt:49 + kt], rstd[:], ALU.mult, ALU.mult)
                C.dma(ov[:, kt, ts_], t_[:])
        else:
            for kt in range(KT):
                C.dma(ov[:, kt, ts_], hT[:, kt, :])
    nw = C.finish([out_d])
    return nc, nw

_CACHE = {}


def _prog(name, fn):
    if name not in _CACHE:
        _CACHE[name] = fn()[0]
    return _CACHE[name]


def _run(nc, maps):
    res = run_bass_kernel_spmd(nc, maps, core_ids=list(range(8)))
    return res.results


def kernel(**inp):
    inp = {k: np.asarray(v) for k, v in inp.items()}
    f = np.float32
    A = np.ascontiguousarray
    h = inp["x"][0].astype(f)
    NCORE = 8
    TS = 1024
    vt = lambda v: A(v.reshape(-1, 128).T)
    for l in range(2):
        ncA = _prog("A", build_A)
        gA = vt(inp["norm_mix_g"][l])
        maps = [{"hT": A(h[c * TS:(c + 1) * TS].T), "g": gA, "w": inp["w_in"][l]} for c in range(NCORE)]
        rA = _run(ncA, maps)
        z = np.concatenate([r["zT"].T for r in rA], 0)
        ncB = _prog("B", build_B)
        rB = _run(ncB, [b_inputs(z, inp, l, c) for c in range(NCORE)])
        hm = np.zeros((8192, 1024), f); yr = np.zeros((8192, 512), f); ys = np.zeros((8192, 512), f)
        for c in range(NCORE):
            hd, half = c // 2, c % 2
            hm[:, hd * 256 + half * 128:hd * 256 + half * 128 + 128] = rB[c]["hm"]
            yr[:, c * 64:(c + 1) * 64] = rB[c]["yr"]
            ys[:, c * 64:(c + 1) * 64] = rB[c]["ys"].T
        final = (l == 1)
        ncC = _prog("C%d" % final, lambda: build_C(final=final))
        vecs = np.zeros((128, 80), f)
        vecs[:, 0:8] = vt(inp["mlstm_norm_g"][l]); vecs[:, 8:12] = vt(inp["s5_glu_b"][l])
        vecs[:, 16:32] = vt(inp["norm_ffn_g"][l]); vecs[:, 32:48] = vt(inp["norm_ple_g"][l]); vecs[:, 48:64] = vt(inp["final_norm_g"])
        maps = []
        for c in range(NCORE):
            sl = slice(c * TS, (c + 1) * TS)
            m = {"hT": A(h[sl].T), "hmT": A(hm[sl].T), "zoT": A(z[sl, 3072:4096].T), "yrT": A(yr[sl].T), "ysT": A(ys[sl].T),
                 "zgT": A(z[sl, G0:].T), "pT": A(inp["p"][l, 0, sl].T), "vecs": vecs}
            for k in ("w_up_m", "w_up_r", "w_up_s", "w_out", "ffn_w_gate", "ffn_w_up", "ffn_w_down", "ple_w_gate", "ple_w_proj"):
                m[k] = inp[k][l]
            m["glu_w"] = inp["s5_glu_w"][l]
            maps.append({k: A(v.astype(f)) for k, v in m.items()})
        rC = _run(ncC, maps)
        h = np.concatenate([r["hout"].T for r in rC], 0)
    return h[None].astype(np.float32)
```
